# Optimizing a Trainium2 kernel written in Bass

```python
import jax, jax.numpy as jnp
from jax import lax
import numpy as np

D_MODEL = 4096
BATCH = 2
SEQ = 4096
DEPTH = 2

N_A = (DEPTH + 1) // 2
N_B = DEPTH - N_A
EPS = 1e-6

MLA_NOPE = 128
MLA_ROPE = 64
MLA_V = 128
MLA_HEADS = D_MODEL // 128
MLA_Q_LORA = 1024
MLA_KV_LORA = 512
ROPE_THETA = 10000.0
Q_BLOCK = 128

SWA_HEAD_DIM = 64
SWA_HEADS = D_MODEL // SWA_HEAD_DIM
SWA_KV_HEADS = 8
WINDOW = 128

D_FF = ((8 * D_MODEL // 3) + 255) // 256 * 256
CONV_W = 3

kernel_name = "yoco_mla_swa_sink_convffn"


def rmsnorm(x, g):
    xf = x.astype(jnp.float32)
    y = xf * lax.rsqrt(jnp.mean(xf * xf, axis=-1, keepdims=True) + EPS)
    return (y * g.astype(jnp.float32)).astype(x.dtype)


def rope(x, pos):
    half = x.shape[-1] // 2
    freqs = ROPE_THETA ** (-jnp.arange(half, dtype=jnp.float32) / half)
    ang = pos.astype(jnp.float32)[..., None] * freqs
    ang = ang.reshape(ang.shape[:2] + (1,) * (x.ndim - 3) + (half,))
    cos, sin = jnp.cos(ang), jnp.sin(ang)
    xf = x.astype(jnp.float32)
    x1, x2 = xf[..., :half], xf[..., half:]
    return jnp.concatenate([x1 * cos - x2 * sin, x2 * cos + x1 * sin], axis=-1).astype(x.dtype)


def mla_attention(h, pos, w_dq, q_norm, w_uq, w_dkv, kv_norm, w_ukv, w_o):
    B, S, _ = h.shape
    H = MLA_HEADS
    cq = rmsnorm(h @ w_dq, q_norm)
    q = (cq @ w_uq).reshape(B, S, H, MLA_NOPE + MLA_ROPE)
    q = jnp.concatenate([q[..., :MLA_NOPE], rope(q[..., MLA_NOPE:], pos)], axis=-1)
    ckv = h @ w_dkv
    c = rmsnorm(ckv[..., :MLA_KV_LORA], kv_norm)
    k_rope = rope(ckv[..., MLA_KV_LORA:], pos)
    kv = (c @ w_ukv).reshape(B, S, H, MLA_NOPE + MLA_V)
    k = jnp.concatenate(
        [kv[..., :MLA_NOPE], jnp.broadcast_to(k_rope[:, :, None, :], (B, S, H, MLA_ROPE))], axis=-1)
    v = kv[..., MLA_NOPE:]
    scale = (MLA_NOPE + MLA_ROPE) ** -0.5
    nb = S // Q_BLOCK
    q_blocks = q.reshape(B, nb, Q_BLOCK, H, MLA_NOPE + MLA_ROPE).transpose(1, 0, 2, 3, 4)
    key_pos = jnp.arange(S)

    def one_block(args):
        qb, i = args
        s = jnp.einsum('bqhd,bkhd->bhqk', qb, k).astype(jnp.float32) * scale
        q_pos = i * Q_BLOCK + jnp.arange(Q_BLOCK)
        causal = key_pos[None, :] <= q_pos[:, None]
        s = jnp.where(causal[None, None], s, -jnp.inf)
        p = jax.nn.softmax(s, axis=-1).astype(v.dtype)
        return jnp.einsum('bhqk,bkhd->bqhd', p, v)

    o = lax.map(one_block, (q_blocks, jnp.arange(nb)))
    o = o.transpose(1, 0, 2, 3, 4).reshape(B, S, H * MLA_V)
    return o @ w_o


def band(x):
    B, S = x.shape[:2]
    nb = S // WINDOW
    xb = x.reshape((B, nb, WINDOW) + x.shape[2:])
    prev = jnp.pad(xb[:, :-1], ((0, 0), (1, 0), (0, 0), (0, 0), (0, 0)))
    return jnp.concatenate([prev, xb], axis=2)


def shared_kv(h, g, w_k, w_v):
    B, S, _ = h.shape
    hn = rmsnorm(h, g)
    k = (hn @ w_k).reshape(B, S, SWA_KV_HEADS, SWA_HEAD_DIM)
    v = (hn @ w_v).reshape(B, S, SWA_KV_HEADS, SWA_HEAD_DIM)
    return band(k), band(v)


def alibi_slopes(n):
    return 2.0 ** (-8.0 * jnp.arange(1, n + 1, dtype=jnp.float32) / n)


def swa_attention(h, k_band, v_band, w_q, sinks, w_o):
    B, S, _ = h.shape
    nb = S // WINDOW
    G = SWA_KV_HEADS
    R = SWA_HEADS // SWA_KV_HEADS
    q = (h @ w_q).reshape(B, nb, WINDOW, G, R, SWA_HEAD_DIM)
    s = jnp.einsum('bnqgrd,bnkgd->bngrqk', q, k_band).astype(jnp.float32) * (SWA_HEAD_DIM ** -0.5)
    qi = jnp.arange(WINDOW)[:, None]
    kj = jnp.arange(2 * WINDOW)[None, :]
    dist = qi + WINDOW - kj
    slopes = alibi_slopes(SWA_HEADS).reshape(G, R)
    s = s - slopes[:, :, None, None] * dist.astype(jnp.float32)
    key_pos = jnp.arange(nb)[:, None, None] * WINDOW - WINDOW + kj[None]
    valid = (dist[None] >= 0) & (dist[None] < WINDOW) & (key_pos >= 0)
    s = jnp.where(valid[None, :, None, None], s, -jnp.inf)
    sink_col = jnp.broadcast_to(sinks.astype(jnp.float32).reshape(G, R)[None, None, :, :, None, None],
                                (B, nb, G, R, WINDOW, 1))
    p = jax.nn.softmax(jnp.concatenate([s, sink_col], axis=-1), axis=-1)[..., :-1].astype(v_band.dtype)
    o = jnp.einsum('bngrqk,bnkgd->bnqgrd', p, v_band).reshape(B, S, SWA_HEADS * SWA_HEAD_DIM)
    return o @ w_o


def conv_ffn(h, w_in, conv_w, conv_b, w_out):
    S = h.shape[1]
    u = h @ w_in
    up = jnp.pad(u, ((0, 0), (CONV_W - 1, 0), (0, 0)))
    u = conv_b + sum(conv_w[j] * up[:, j:j + S] for j in range(CONV_W))
    gate, val = u[..., :D_FF], u[..., D_FF:]
    return (jax.nn.silu(gate) * val) @ w_out


def setup_inputs(seed: int = 0) -> dict:
    key = jax.random.key(seed)
    ks = jax.random.split(key, 32)
    f32 = jnp.float32

    def w(k, shape, fan_in):
        return jax.random.normal(k, shape, f32) * (fan_in ** -0.5)

    def gain(k, shape):
        return 1.0 + 0.05 * jax.random.normal(k, shape, f32)

    D = D_MODEL
    return {
        "x": jax.random.normal(ks[0], (BATCH, SEQ, D), f32),
        "positions": (jnp.arange(SEQ, dtype=jnp.int32)[None, :]
                      + jax.random.randint(ks[1], (BATCH, 1), 0, 1024, dtype=jnp.int32)),
        "norm_mix_pre": gain(ks[2], (DEPTH, D)),
        "norm_mix_post": gain(ks[3], (DEPTH, D)),
        "norm_ffn_pre": gain(ks[4], (DEPTH, D)),
        "norm_ffn_post": gain(ks[5], (DEPTH, D)),
        "mla_w_dq": w(ks[6], (N_A, D, MLA_Q_LORA), D),
        "mla_q_norm": gain(ks[7], (N_A, MLA_Q_LORA)),
        "mla_w_uq": w(ks[8], (N_A, MLA_Q_LORA, MLA_HEADS * (MLA_NOPE + MLA_ROPE)), MLA_Q_LORA),
        "mla_w_dkv": w(ks[9], (N_A, D, MLA_KV_LORA + MLA_ROPE), D),
        "mla_kv_norm": gain(ks[10], (N_A, MLA_KV_LORA)),
        "mla_w_ukv": w(ks[11], (N_A, MLA_KV_LORA, MLA_HEADS * (MLA_NOPE + MLA_V)), MLA_KV_LORA),
        "mla_w_o": w(ks[12], (N_A, MLA_HEADS * MLA_V, D), MLA_HEADS * MLA_V),
        "shared_kv_norm": gain(ks[13], (D,)),
        "swa_w_k": w(ks[14], (D, SWA_KV_HEADS * SWA_HEAD_DIM), D),
        "swa_w_v": w(ks[15], (D, SWA_KV_HEADS * SWA_HEAD_DIM), D),
        "swa_w_q": w(ks[16], (N_B, D, SWA_HEADS * SWA_HEAD_DIM), D),
        "swa_sinks": 0.5 * jax.random.normal(ks[17], (N_B, SWA_HEADS), f32),
        "swa_w_o": w(ks[18], (N_B, SWA_HEADS * SWA_HEAD_DIM, D), SWA_HEADS * SWA_HEAD_DIM),
        "ffn_w_in": w(ks[19], (DEPTH, D, 2 * D_FF), D),
        "ffn_conv_w": w(ks[20], (DEPTH, CONV_W, 2 * D_FF), CONV_W),
        "ffn_conv_b": 0.01 * jax.random.normal(ks[21], (DEPTH, 2 * D_FF), f32),
        "ffn_w_out": w(ks[22], (DEPTH, D_FF, D), D_FF),
    }


def reference(x, positions, norm_mix_pre, norm_mix_post, norm_ffn_pre, norm_ffn_post,
              mla_w_dq, mla_q_norm, mla_w_uq, mla_w_dkv, mla_kv_norm, mla_w_ukv, mla_w_o,
              shared_kv_norm, swa_w_k, swa_w_v, swa_w_q, swa_sinks, swa_w_o,
              ffn_w_in, ffn_conv_w, ffn_conv_b, ffn_w_out):
    h = x
    k_band = None
    v_band = None
    for l in range(DEPTH):
        hn = rmsnorm(h, norm_mix_pre[l])
        if l < N_A:
            a = mla_attention(hn, positions, mla_w_dq[l], mla_q_norm[l], mla_w_uq[l],
                              mla_w_dkv[l], mla_kv_norm[l], mla_w_ukv[l], mla_w_o[l])
        else:
            if l == N_A:
                k_band, v_band = shared_kv(h, shared_kv_norm, swa_w_k, swa_w_v)
            j = l - N_A
            a = swa_attention(hn, k_band, v_band, swa_w_q[j], swa_sinks[j], swa_w_o[j])
        h = h + rmsnorm(a, norm_mix_post[l])
        f = conv_ffn(rmsnorm(h, norm_ffn_pre[l]), ffn_w_in[l], ffn_conv_w[l], ffn_conv_b[l], ffn_w_out[l])
        h = h + rmsnorm(f, norm_ffn_post[l])
    return h
```

```python
import contextlib
import math
import numpy as np
import concourse.bass as bass
import concourse.mybir as mybir
from concourse.bass_utils import run_bass_kernel_spmd

F32 = mybir.dt.float32
BF16 = mybir.dt.bfloat16
I32 = mybir.dt.int32
ALU = mybir.AluOpType
AF = mybir.ActivationFunctionType

ENGS = ['pe', 'act', 'dve', 'pool', 'sp']

D = 4096
KC = 32
SEQ = 4096
CH = 1024
HALO = 132
T0 = CH + HALO
T1 = T0 - 128
OWN1 = HALO - 128
NK = 4096
EXT0 = CH - HALO
DFF = 11008
NJ = DFF // 128
EPS = 1e-6
TB0 = [(i * 289, 289) for i in range(4)]
TB1 = [(i * 257, 257) for i in range(4)]
NEG = -30000.0


class Buf:
    __slots__ = ('name', 'w', 'rs', 'excl')

    def __init__(self, name):
        self.name = name
        self.w = None
        self.rs = {}
        self.excl = False


class Rot:
    def __init__(self, items):
        self.items = list(items)
        self.i = 0

    def next(self):
        v = self.items[self.i % len(self.items)]
        self.i += 1
        return v


class Prog:
    def __init__(self, nc):
        self.nc = nc
        self.ops = {e: [] for e in ENGS}
        self.chan_cnt = {}
        self.nbuf = 0

    def buf(self, name=None):
        self.nbuf += 1
        return Buf(name or f"b{self.nbuf}")

    def op(self, eng, fn, reads=(), writes=(), dma=None):
        idx = len(self.ops[eng])
        deps = {}

        def add(ev, same_ok):
            if ev is None:
                return
            k, i = ev
            if k == eng and same_ok:
                return
            if deps.get(k, -1) < i:
                deps[k] = i
        for b in reads:
            add(b.w, False)
            if b.excl:
                for k, i in b.rs.items():
                    add((k, i), True)
        for b in writes:
            add(b.w, True)
            for k, i in b.rs.items():
                add((k, i), True)
        if dma is not None:
            ck = 'ch:' + dma
            n = self.chan_cnt.get(dma, 0)
            if n > 0:
                add((ck, n - 1), False)
            self.chan_cnt[dma] = n + 1
            ev = (ck, n)
        else:
            ev = (eng, idx)
        self.ops[eng].append(dict(fn=fn, deps=deps, dma=dma, need=False))
        ws = set(id(b) for b in writes)
        for b in reads:
            if id(b) not in ws:
                if b.rs.get(ev[0], -1) < ev[1]:
                    b.rs[ev[0]] = ev[1]
        for b in writes:
            b.w = ev
            b.rs = {}
        return ev

    def wait_events(self, eng, evs):
        deps = {}
        for k, i in evs:
            if deps.get(k, -1) < i:
                deps[k] = i
        self.ops[eng].append(dict(fn=None, deps=deps, dma=None, need=False))

    def _last_events(self):
        evs = []
        for ch, n in self.chan_cnt.items():
            evs.append(('ch:' + ch, n - 1))
        for e in ENGS:
            for i in range(len(self.ops[e]) - 1, -1, -1):
                o = self.ops[e][i]
                if o['dma'] is None and o['fn'] is not None:
                    evs.append((e, i))
                    break
        return evs

    def barrier(self):
        evs = self._last_events()
        for e in ENGS:
            self.wait_events(e, [ev for ev in evs if ev[0] != e])

    def finish(self, eng='sp'):
        self.wait_events(eng, [ev for ev in self._last_events() if ev[0] != eng])

    def emit(self):
        nc = self.nc
        for e in ENGS:
            for o in self.ops[e]:
                for k, i in o['deps'].items():
                    if not k.startswith('ch:'):
                        self.ops[k][i]['need'] = True
        val = {}
        for e in ENGS:
            c = 0
            v = []
            for o in self.ops[e]:
                if o['need']:
                    c += 1
                v.append(c)
            val[e] = v
        with contextlib.ExitStack() as st:
            sems = {}
            for e in ENGS:
                sems[e] = st.enter_context(nc.semaphore("s_" + e))
            for ch in self.chan_cnt:
                sems['ch:' + ch] = st.enter_context(nc.semaphore("c_" + ch))
            block = st.enter_context(nc.Block())
            engmap = {'pe': block.tensor, 'act': block.scalar, 'dve': block.vector,
                      'pool': block.gpsimd, 'sp': block.sync}

            def make(e):
                def body(engine):
                    waited = {}
                    for o in self.ops[e]:
                        for k, i in o['deps'].items():
                            v = 16 * (i + 1) if k.startswith('ch:') else val[k][i]
                            if waited.get(k, 0) < v:
                                engine.wait_ge(sems[k], v)
                                waited[k] = v
                        if o['fn'] is None:
                            continue
                        ins = o['fn'](engine)
                        if o['dma'] is not None:
                            ins.then_inc(sems['ch:' + o['dma']], 16)
                        elif o['need']:
                            ins.then_inc(sems[e], 1)
                return body
            for e in ENGS:
                if self.ops[e]:
                    engmap[e](make(e))


class K:
    pass


def build(dbg=(), stop_after=None, only=None, variant=0):
    nc = bass.Bass("TRN2", target_bir_lowering=False)
    P = Prog(nc)
    k = K()
    k.nc, k.P = nc, P

    def din(name, shape, dt=F32):
        return nc.dram_tensor(name, list(shape), dt, kind="ExternalInput")

    def dscr(name, shape, dt):
        kind = "ExternalOutput" if name in dbg else "Internal"
        return nc.dram_tensor(name, list(shape), dt, kind=kind)

    xk = din("xk", [NK // 128, 128, KC * 128])
    pos = din("pos", [NK], I32)
    kbias0 = din("kbias0", [128, 32])
    kbias1 = din("kbias1", [128, 10])
    flag = din("flag", [128, 1])
    ropec = din("ropec", [64, 4])
    masks = din("masks", [128, 128 + 2 * HALO], BF16)
    eh = din("eh", [32, 128, 2, 256])
    sinks = din("sinks", [128, 32])
    gains = din("gains", [128, 9, 32])
    qn_g = din("qn_g", [128, 8])
    kvn_g = din("kvn_g", [128, 4])
    w_dkv = din("w_dkv", [128, KC * 640])
    w_dq = din("w_dq", [8, 128, KC * 128])
    w_uq = din("w_uq", [32, 128, 8 * 256])
    w_ukv = din("w_ukv", [32, 128, 4 * 256])
    w_o0 = din("w_o0", [32, 128, KC * 128])
    w_k = din("w_k", [8, 128, KC * 128])
    w_v = din("w_v", [128, KC * 512])
    w_q = din("w_q", [32, 128, KC * 128])
    w_o1 = din("w_o1", [32, 128, KC * 128])
    PH = ["A1", "A2", "A3", "A4", "A5", "F1a", "F2a", "F3a", "B1", "B2", "B3", "B4", "F1b", "F2b", "F3b"]
    last = PH.index(stop_after) if stop_after else len(PH) - 1
    run_list = list(only) if only else PH[:last + 1]
    need_ffn = [any(p_ in run_list for p_ in (("F1a", "F2a"), ("F1b", "F2b"))[l]) for l in range(2)]
    w_in = [din(f"w_in{l}", [NJ, 128, 2 * KC * 128]) if need_ffn[l] else None for l in range(2)]
    w_out = [din(f"w_out{l}", [32, 128, NJ * 128]) if need_ffn[l] else None for l in range(2)]
    cw = [din(f"cw{l}", [128, 3, 2 * NJ]) for l in range(2)]
    cb = [din(f"cb{l}", [128, 2 * NJ]) for l in range(2)]
    out_d = nc.dram_tensor("out", [D, CH], F32, kind="ExternalOutput")

    HN0 = dscr("HN0", [D, T0], BF16)
    CT = dscr("CT", [512, NK], BF16)
    KR = dscr("KR", [64, NK], BF16)
    CSQ = dscr("CSQ", [2, 64, T0], F32)
    CQ = dscr("CQ", [1024, T0], BF16)
    AO = dscr("AO", [D, T0], BF16)
    Y = dscr("Y", [D, T0], F32)
    HM0 = dscr("HM0", [D, T0], F32)
    HN = dscr("HN", [D, T0], BF16)
    ACT = dscr("ACT", [NJ, 128, T0], BF16)
    H1 = dscr("H1", [D, T0], F32)
    HNQ = dscr("HNQ", [D, T0], BF16)
    HKV = dscr("HKV", [D, T0], BF16)
    KD = dscr("KD", [8, 128, T0], BF16)
    VL = dscr("VL", [10, 128, 8 * 128], BF16)
    VR = dscr("VR", [10, 128, 8 * 128], BF16)
    HM1 = dscr("HM1", [D, T1], F32)
    b_d = {n: P.buf("d_" + n) for n in ["HN0", "CT", "KR", "CSQ", "CQ", "AO", "Y", "HM0", "HN", "ACT", "H1",
                                       "HNQ", "HKV", "KD", "VL", "VR", "HM1", "out"]}

    ARENA_W = 51200
    arena = nc.alloc_sbuf_tensor("arena", [128, ARENA_W], F32)
    ps = [nc.alloc_psum_tensor(f"ps{i}", [128, 512], F32) for i in range(8)]
    b_ps = [P.buf(f"ps{i}") for i in range(8)]
    for b_ in b_ps:
        b_.excl = True

    class Carver:
        def __init__(self):
            self.off = 0

        def get(self, shape, dt, parts=128):
            n = int(np.prod(shape[1:]))
            nb = n * (4 if dt in (F32, I32) else 2)
            nw = (nb + 3) // 4
            nw = (nw + 7) // 8 * 8
            assert self.off + nw <= ARENA_W, (self.off, nw, shape)
            v = arena[0:shape[0], self.off:self.off + nw]
            self.off += nw
            if dt != F32:
                v = v.bitcast(dt)
            v = v[:, 0:n]
            if len(shape) == 3:
                v = v.rearrange("p (a b) -> p a b", a=shape[1])
            elif len(shape) == 4:
                v = v.rearrange("p (a b c) -> p a b c", a=shape[1], b=shape[2])
            return v

    CV = Carver()
    ones_bf = CV.get([128, 128], BF16)
    ones_f = CV.get([128, 128], F32)
    onesL = CV.get([128, 128], BF16)
    onesR = CV.get([128, 128], BF16)
    g_sb = CV.get([128, 9, 32], F32)
    qn_sb = CV.get([128, 8], F32)
    kvn_sb = CV.get([128, 4], F32)
    flag_sb = CV.get([128, 1], F32)
    kb0_sb = CV.get([128, 32], F32)
    kb1_sb = CV.get([128, 10], F32)
    mask_sb = CV.get([128, 128 + 2 * HALO], BF16)
    esink = CV.get([128, 32], F32)
    ropec_sb = CV.get([64, 4], F32)
    eps_sb = CV.get([128, 1], F32)
    b_const = P.buf("const")
    BASE = CV.off

    def carver():
        c = Carver()
        c.off = BASE
        return c

    def ld(eng, out, in_, writes, ch, reads=()):
        P.op(eng, lambda e: e.dma_start(out=out, in_=in_), reads=list(reads), writes=list(writes), dma=ch)

    ld('sp', g_sb, gains.ap(), [b_const], 'c0')
    ld('sp', qn_sb, qn_g.ap(), [b_const], 'c0')
    ld('sp', kvn_sb, kvn_g.ap(), [b_const], 'c0')
    ld('sp', flag_sb, flag.ap(), [b_const], 'c0')
    ld('sp', kb0_sb, kbias0.ap(), [b_const], 'c0')
    ld('sp', kb1_sb, kbias1.ap(), [b_const], 'c0')
    ld('sp', mask_sb, masks.ap(), [b_const], 'c0')
    ld('sp', esink, sinks.ap(), [b_const], 'c0')
    ld('sp', ropec_sb, ropec.ap(), [b_const], 'c0')
    P.op('dve', lambda e: e.memset(ones_bf, 1.0), writes=[b_const])
    P.op('dve', lambda e: e.memset(ones_f, 1.0), writes=[b_const])
    P.op('dve', lambda e: e.memset(eps_sb, EPS), writes=[b_const])
    P.op('dve', lambda e: e.memset(onesL, 0.0), writes=[b_const])
    P.op('dve', lambda e: e.memset(onesR, 0.0), writes=[b_const])
    P.op('dve', lambda e: e.memset(onesL[:, 0:64], 1.0), writes=[b_const])
    P.op('dve', lambda e: e.memset(onesR[:, 64:128], 1.0), writes=[b_const])
    P.op('act', lambda e: e.activation(out=esink, in_=esink, func=AF.Exp), reads=[b_const], writes=[b_const])
    P.barrier()

    def rstd_from_ps(bank, np_, w, out, b_out, inv_n):
        P.op('act', lambda e: e.activation(out=out, in_=ps[bank][0:np_, 0:w], func=AF.Sqrt, bias=eps_sb[0:np_, 0:1], scale=inv_n),
             reads=[b_ps[bank], b_const], writes=[b_out])
        P.op('dve', lambda e: e.reciprocal(out=out, in_=out), reads=[b_out], writes=[b_out])

    def sumsq(src3, b_src, sq3, b_sq, nch, w, bank):
        P.op('act', lambda e: e.activation(out=sq3, in_=src3, func=AF.Square), reads=[b_src], writes=[b_sq])
        for c_ in range(nch):
            P.op('pe', lambda e, c_=c_: e.matmul(ps[bank][:, 0:w], lhsT=ones_bf, rhs=sq3[:, c_, :],
                                                 start=(c_ == 0), stop=(c_ == nch - 1)),
                 reads=[b_sq, b_const], writes=[b_ps[bank]])

    def stream_gemm(wd, nblk, groups, kc_n, M, rhs_fn, b_x, tbs, epi, post, rot, slots, tag):
        for j in range(nblk):
            wv, b_w, ch = slots.next()
            P.op('pool', lambda e, j=j, wv=wv: e.dma_start(
                out=wv.rearrange("p g k m -> p (g k m)"), in_=wd[j, :, :]), writes=[b_w], dma=ch)
            for g in range(groups):
                for (t0, w) in tbs:
                    bank = rot.next()
                    for kc in range(kc_n):
                        P.op('pe', lambda e, bank=bank, wv=wv, g=g, kc=kc, t0=t0, w=w: e.matmul(
                            ps[bank][0:M, 0:w], lhsT=wv[:, g, kc, :], rhs=rhs_fn(kc, t0, w),
                            start=(kc == 0), stop=(kc == kc_n - 1)),
                            reads=[b_w, b_x], writes=[b_ps[bank]])
                    epi(j, g, t0, w, bank)
            if post is not None:
                post(j)

    def mk_slots(C, n, groups, kc_n, M, tag):
        return Rot([(C.get([128, groups, kc_n, M], BF16), P.buf(f"w{tag}{i}"), f"w{i}") for i in range(n)])

    def load_xT(C, src, T, col0=0):
        xT = C.get([128, KC, T], BF16)
        b_x = P.buf("xT")
        v = src.ap().rearrange("(k p) t -> p k t", p=128)
        for q4 in range(4):
            ld('sp', xT[:, q4 * 8:(q4 + 1) * 8, :], v[:, q4 * 8:(q4 + 1) * 8, col0:col0 + T], [b_x], f'x{q4}',
               reads=[b_d[src.name]] if src.name in b_d else [])
        return xT, b_x

    rot8 = Rot(range(8))

    def phase_A1():
        C = carver()
        wdkv = C.get([128, KC, 640], BF16)
        b_wdkv = P.buf("wdkv")
        P.op('pool', lambda e: e.dma_start(out=wdkv.rearrange("p k m -> p (k m)"), in_=w_dkv.ap()),
             writes=[b_wdkv], dma='w0')
        posi = C.get([64, NK], I32)
        CC = C.get([64, NK], F32)
        SS = C.get([64, NK], F32)
        t_rnd = C.get([64, NK], F32)
        b_tab = P.buf("tab")
        ld('sp', posi, pos.ap().partition_broadcast(64), [b_tab], 'x0')
        P.op('dve', lambda e: e.tensor_copy(out=CC, in_=posi), reads=[b_tab], writes=[b_tab])
        P.op('dve', lambda e: e.tensor_scalar(out=SS, in0=CC, scalar1=ropec_sb[:, 0:1], scalar2=ropec_sb[:, 2:3],
                                              op0=ALU.mult, op1=ALU.add), reads=[b_tab, b_const], writes=[b_tab])
        P.op('dve', lambda e: e.tensor_scalar(out=CC, in0=CC, scalar1=ropec_sb[:, 0:1], scalar2=ropec_sb[:, 1:2],
                                              op0=ALU.mult, op1=ALU.add), reads=[b_tab, b_const], writes=[b_tab])
        posf = posi.bitcast(F32)
        for tb_ in (CC, SS):
            P.op('dve', lambda e, tb_=tb_: e.tensor_copy(out=posi, in_=tb_), reads=[b_tab], writes=[b_tab])
            P.op('dve', lambda e, tb_=tb_: e.tensor_copy(out=t_rnd, in_=posi), reads=[b_tab], writes=[b_tab])
            P.op('dve', lambda e, tb_=tb_: e.tensor_tensor(out=tb_, in0=tb_, in1=t_rnd, op=ALU.subtract), reads=[b_tab], writes=[b_tab])
            P.op('act', lambda e, tb_=tb_: e.activation(out=tb_, in_=tb_, func=AF.Sin, scale=2.0 * math.pi * 0.999999),
                 reads=[b_tab], writes=[b_tab])
        ld('sp', CSQ[0, :, :], CC[:, EXT0:EXT0 + T0], [b_d["CSQ"]], 'st0', reads=[b_tab])
        ld('sp', CSQ[1, :, :], SS[:, EXT0:EXT0 + T0], [b_d["CSQ"]], 'st1', reads=[b_tab])
        W = 128
        xb = [C.get([128, KC, W], F32) for _ in range(2)]
        b_xb = [P.buf() for _ in range(2)]
        sqhn = [C.get([128, KC, W], BF16) for _ in range(2)]
        b_sqhn = [P.buf() for _ in range(2)]
        rstd_ = [C.get([128, W], F32) for _ in range(2)]
        b_rstd_ = [P.buf() for _ in range(2)]
        ckv_ = [C.get([128, 4, W], F32) for _ in range(2)]
        b_ckv_ = [P.buf() for _ in range(2)]
        sq2_ = [C.get([128, 4, W], BF16) for _ in range(2)]
        b_sq2_ = [P.buf() for _ in range(2)]
        rstd2_ = [C.get([128, W], F32) for _ in range(2)]
        b_rstd2_ = [P.buf() for _ in range(2)]
        cst = [C.get([128, 4, W], BF16) for _ in range(2)]
        b_cst = [P.buf() for _ in range(2)]
        krs = [C.get([64, W], BF16) for _ in range(2)]
        b_krs = [P.buf() for _ in range(2)]
        t1_ = [C.get([64, W], F32) for _ in range(2)]
        t2_ = [C.get([64, W], F32) for _ in range(2)]
        b_t_ = [P.buf() for _ in range(2)]
        NB = NK // W

        def stage1(ib):
            s0 = ib * W
            i2 = ib % 2
            X, bX, H_, bH = xb[i2], b_xb[i2], sqhn[i2], b_sqhn[i2]
            rstd, b_rstd = rstd_[i2], b_rstd_[i2]
            ld('sp', X[:, 0:16, :], xk[ib, :, 0:2048].rearrange("p (k w) -> p k w", k=16), [bX], f'x{1 + 2 * i2}')
            ld('sp', X[:, 16:32, :], xk[ib, :, 2048:4096].rearrange("p (k w) -> p k w", k=16), [bX], f'x{2 + 2 * i2}')
            bk = rot8.next()
            sumsq(X, bX, H_, bH, KC, W, bk)
            rstd_from_ps(bk, 128, W, rstd, b_rstd, 1.0 / D)
            P.op('dve', lambda e: e.tensor_tensor(out=X, in0=X, in1=g_sb[:, 0, :].unsqueeze(2).broadcast_to([128, KC, W]),
                                                   op=ALU.mult), reads=[bX, b_const], writes=[bX])
            P.op('dve', lambda e: e.tensor_tensor(out=H_, in0=X, in1=rstd.unsqueeze(1).broadcast_to([128, KC, W]),
                                                  op=ALU.mult), reads=[bX, b_rstd], writes=[bH])
            a, b = max(s0, EXT0), min(s0 + W, EXT0 + T0)
            if a < b:
                ld('sp', HN0.ap().rearrange("(k p) t -> p k t", p=128)[:, :, a - EXT0:b - EXT0], H_[:, :, a - s0:b - s0],
                   [b_d["HN0"]], f'st{i2}', reads=[bH])

        def stage2a(ib):
            s0 = ib * W
            i2 = ib % 2
            H_, bH = sqhn[i2], b_sqhn[i2]
            ckv, b_ckv = ckv_[i2], b_ckv_[i2]
            t1, t2, b_t = t1_[i2], t2_[i2], b_t_[i2]
            for m in range(4):
                bk = rot8.next()
                for kc in range(KC):
                    P.op('pe', lambda e, bk=bk, m=m, kc=kc: e.matmul(
                        ps[bk][:, 0:W], lhsT=wdkv[:, kc, m * 128:(m + 1) * 128], rhs=H_[:, kc, :],
                        start=(kc == 0), stop=(kc == KC - 1)), reads=[b_wdkv, bH], writes=[b_ps[bk]])
                P.op('act', lambda e, bk=bk, m=m: e.copy(out=ckv[:, m, :], in_=ps[bk][:, 0:W]),
                     reads=[b_ps[bk]], writes=[b_ckv])
            bkA, bkB = rot8.next(), rot8.next()
            for (bk, c0) in ((bkA, 512), (bkB, 576)):
                for kc in range(KC):
                    P.op('pe', lambda e, bk=bk, c0=c0, kc=kc: e.matmul(
                        ps[bk][0:64, 0:W], lhsT=wdkv[:, kc, c0:c0 + 64], rhs=H_[:, kc, :],
                        start=(kc == 0), stop=(kc == KC - 1)), reads=[b_wdkv, bH], writes=[b_ps[bk]])
            P.op('dve', lambda e: e.tensor_tensor(out=t1, in0=ps[bkA][0:64, 0:W], in1=CC[:, s0:s0 + W],
                                                  op=ALU.mult), reads=[b_ps[bkA], b_tab], writes=[b_t])
            P.op('dve', lambda e: e.tensor_tensor(out=t2, in0=ps[bkB][0:64, 0:W], in1=SS[:, s0:s0 + W],
                                                  op=ALU.mult), reads=[b_ps[bkB], b_tab, b_t], writes=[b_t])
            P.op('dve', lambda e: e.tensor_tensor(out=krs[i2], in0=t1, in1=t2, op=ALU.add),
                 reads=[b_t], writes=[b_krs[i2]])
            ld('sp', KR[:, s0:s0 + W], krs[i2], [b_d["KR"]], f'st{2 + i2}', reads=[b_krs[i2]])

        def stage2b(ib):
            s0 = ib * W
            i2 = ib % 2
            ckv, b_ckv = ckv_[i2], b_ckv_[i2]
            sq2, b_sq2, rstd2, b_rstd2 = sq2_[i2], b_sq2_[i2], rstd2_[i2], b_rstd2_[i2]
            bk = rot8.next()
            sumsq(ckv, b_ckv, sq2, b_sq2, 4, W, bk)
            rstd_from_ps(bk, 128, W, rstd2, b_rstd2, 1.0 / 512)
            P.op('dve', lambda e: e.tensor_tensor(out=ckv, in0=ckv, in1=kvn_sb.unsqueeze(2).broadcast_to([128, 4, W]),
                                                  op=ALU.mult), reads=[b_ckv, b_const], writes=[b_ckv])
            P.op('dve', lambda e: e.tensor_tensor(out=cst[i2], in0=ckv, in1=rstd2.unsqueeze(1).broadcast_to([128, 4, W]),
                                                  op=ALU.mult), reads=[b_ckv, b_rstd2], writes=[b_cst[i2]])
            ld('sp', CT.ap().rearrange("(k p) t -> p k t", p=128)[:, :, s0:s0 + W], cst[i2], [b_d["CT"]], f'st{4 + i2}',
               reads=[b_cst[i2]])

        stage1(0)
        for ib in range(NB):
            if ib + 1 < NB:
                stage1(ib + 1)
            stage2a(ib)
            if ib >= 1:
                stage2b(ib - 1)
        stage2b(NB - 1)
        P.barrier()

    def phase_A2():
        C = carver()
        slots = mk_slots(C, 3, 1, KC, 128, "a2")
        xT, b_x = load_xT(C, HN0, T0)
        cqf = C.get([128, 8, T0], F32)
        b_cqf = P.buf()

        def epi(j, g, t0, w, bank):
            P.op('act', lambda e: e.copy(out=cqf[:, j, t0:t0 + w], in_=ps[bank][:, 0:w]), reads=[b_ps[bank]], writes=[b_cqf])
        stream_gemm(w_dq, 8, 1, KC, 128, lambda kc, t0, w: xT[:, kc, t0:t0 + w], b_x, TB0, epi, None, rot8, slots, "a2")
        sq = C.get([128, 8, 289], BF16)
        b_sq = P.buf()
        rstd = C.get([128, 289], F32)
        b_rstd = P.buf()
        cqb = C.get([128, 8, T0], BF16)
        b_cqb = P.buf()
        for (t0, w) in TB0:
            bk = rot8.next()
            sumsq(cqf[:, :, t0:t0 + w], b_cqf, sq, b_sq, 8, w, bk)
            rstd_from_ps(bk, 128, w, rstd, b_rstd, 1.0 / 1024)
            P.op('dve', lambda e, t0=t0, w=w: e.tensor_tensor(
                out=cqf[:, :, t0:t0 + w], in0=cqf[:, :, t0:t0 + w], in1=qn_sb.unsqueeze(2).broadcast_to([128, 8, w]),
                op=ALU.mult), reads=[b_cqf, b_const], writes=[b_cqf])
            P.op('dve', lambda e, t0=t0, w=w: e.tensor_tensor(
                out=cqb[:, :, t0:t0 + w], in0=cqf[:, :, t0:t0 + w], in1=rstd.unsqueeze(1).broadcast_to([128, 8, w]),
                op=ALU.mult), reads=[b_cqf, b_rstd], writes=[b_cqb])
        ld('sp', CQ.ap().rearrange("(k p) t -> p k t", p=128), cqb, [b_d["CQ"]], 'st0', reads=[b_cqb])
        P.barrier()

    def phase_A3(nheads=32):
        C = carver()
        cT = C.get([128, 4, NK], BF16)
        krT = C.get([64, NK], BF16)
        cqT = C.get([128, 8, T0], BF16)
        CCq = C.get([64, T0], F32)
        SSq = C.get([64, T0], F32)
        b_res = P.buf("a3res")
        ld('sp', cT, CT.ap().rearrange("(k p) t -> p k t", p=128), [b_res], 'x0', reads=[b_d["CT"]])
        ld('sp', krT, KR.ap(), [b_res], 'x1', reads=[b_d["KR"]])
        ld('sp', cqT, CQ.ap().rearrange("(k p) t -> p k t", p=128), [b_res], 'x2', reads=[b_d["CQ"]])
        ld('sp', CCq, CSQ[0, :, :], [b_res], 'x3', reads=[b_d["CSQ"]])
        ld('sp', SSq, CSQ[1, :, :], [b_res], 'x4', reads=[b_d["CSQ"]])
        wq = [C.get([128, 8, 256], BF16) for _ in range(2)]
        b_wq = [P.buf() for _ in range(2)]
        wkv = [C.get([128, 4, 256], BF16) for _ in range(2)]
        b_wkv = [P.buf() for _ in range(2)]
        qn = [C.get([128, T0], BF16) for _ in range(2)]
        b_qn = [P.buf() for _ in range(2)]
        qr = [C.get([64, T0], BF16) for _ in range(2)]
        b_qr = [P.buf() for _ in range(2)]
        knT = [C.get([128, NK], BF16) for _ in range(2)]
        b_kn = [P.buf() for _ in range(2)]
        vsb = [C.get([128, 32, 128], BF16) for _ in range(2)]
        b_v = [P.buf() for _ in range(2)]
        pT = [C.get([128, 512], BF16) for _ in range(4)]
        b_pT = [P.buf() for _ in range(4)]
        aoh = [C.get([128, T0], BF16) for _ in range(2)]
        b_aoh = [P.buf() for _ in range(2)]
        rl = [C.get([128, 512], F32) for _ in range(2)]
        b_rl = [P.buf() for _ in range(2)]
        lacc = [C.get([128, 512], F32) for _ in range(2)]
        b_lacc = [P.buf() for _ in range(2)]
        t1 = C.get([64, T0], F32)
        t2 = C.get([64, T0], F32)
        b_t = P.buf()
        rot4 = Rot(range(4))
        prot = Rot(range(4))
        SCALE = 192 ** -0.5
        QBS = [(0, HALO), (HALO, 512), (HALO + 512, 512)]
        TRI = mask_sb[:, 0:128]
        M6 = mask_sb[:, 128:128 + HALO]
        M7 = mask_sb[:, 128 + HALO:128 + 2 * HALO]
        qbi = 0
        def load_head_w(h):
            hb = h % 2
            P.op('pool', lambda e: e.dma_start(out=wq[hb].rearrange("p k m -> p (k m)"), in_=w_uq[h, :, :]),
                 writes=[b_wq[hb]], dma=f'w{hb}')
            P.op('pool', lambda e: e.dma_start(out=wkv[hb].rearrange("p k m -> p (k m)"), in_=w_ukv[h, :, :]),
                 writes=[b_wkv[hb]], dma=f'w{2 + hb}')
        load_head_w(0)
        for h in range(nheads):
            hb = h % 2
            for (t0, w) in TB0:
                bk = rot4.next()
                for kc in range(8):
                    P.op('pe', lambda e, bk=bk, kc=kc, t0=t0, w=w, hb=hb: e.matmul(
                        ps[bk][:, 0:w], lhsT=wq[hb][:, kc, 0:128], rhs=cqT[:, kc, t0:t0 + w],
                        start=(kc == 0), stop=(kc == 7)), reads=[b_wq[hb], b_res], writes=[b_ps[bk]])
                P.op('act', lambda e, bk=bk, t0=t0, w=w, hb=hb: e.copy(out=qn[hb][:, t0:t0 + w], in_=ps[bk][:, 0:w]),
                     reads=[b_ps[bk]], writes=[b_qn[hb]])
            for (t0, w) in TB0:
                bkA, bkB = rot4.next(), rot4.next()
                for (bk, c0) in ((bkA, 128), (bkB, 192)):
                    for kc in range(8):
                        P.op('pe', lambda e, bk=bk, c0=c0, kc=kc, t0=t0, w=w, hb=hb: e.matmul(
                            ps[bk][0:64, 0:w], lhsT=wq[hb][:, kc, c0:c0 + 64], rhs=cqT[:, kc, t0:t0 + w],
                            start=(kc == 0), stop=(kc == 7)), reads=[b_wq[hb], b_res], writes=[b_ps[bk]])
                P.op('dve', lambda e, bkA=bkA, t0=t0, w=w: e.tensor_tensor(
                    out=t1[:, t0:t0 + w], in0=ps[bkA][0:64, 0:w], in1=CCq[:, t0:t0 + w], op=ALU.mult),
                    reads=[b_ps[bkA], b_res], writes=[b_t])
                P.op('dve', lambda e, bkB=bkB, t0=t0, w=w: e.tensor_tensor(
                    out=t2[:, t0:t0 + w], in0=ps[bkB][0:64, 0:w], in1=SSq[:, t0:t0 + w], op=ALU.mult),
                    reads=[b_ps[bkB], b_res, b_t], writes=[b_t])
            P.op('dve', lambda e, hb=hb: e.tensor_tensor(out=qr[hb], in0=t1, in1=t2, op=ALU.add),
                 reads=[b_t], writes=[b_qr[hb]])
            for kb in range(8):
                bk = rot4.next()
                for kc in range(4):
                    P.op('pe', lambda e, bk=bk, kc=kc, kb=kb, hb=hb: e.matmul(
                        ps[bk][:, 0:512], lhsT=wkv[hb][:, kc, 0:128], rhs=cT[:, kc, kb * 512:(kb + 1) * 512],
                        start=(kc == 0), stop=(kc == 3)), reads=[b_wkv[hb], b_res], writes=[b_ps[bk]])
                eng = 'act' if kb % 2 == 0 else 'dve'
                if eng == 'act':
                    P.op('act', lambda e, bk=bk, kb=kb, hb=hb: e.copy(out=knT[hb][:, kb * 512:(kb + 1) * 512], in_=ps[bk][:, 0:512]),
                         reads=[b_ps[bk]], writes=[b_kn[hb]])
                else:
                    P.op('dve', lambda e, bk=bk, kb=kb, hb=hb: e.tensor_copy(out=knT[hb][:, kb * 512:(kb + 1) * 512], in_=ps[bk][:, 0:512]),
                         reads=[b_ps[bk]], writes=[b_kn[hb]])
            for k4 in range(8):
                bk = rot4.next()
                for i4 in range(4):
                    kt = k4 * 4 + i4
                    for kc in range(4):
                        P.op('pe', lambda e, bk=bk, kc=kc, kt=kt, i4=i4, hb=hb: e.matmul(
                            ps[bk][:, i4 * 128:(i4 + 1) * 128], lhsT=cT[:, kc, kt * 128:(kt + 1) * 128], rhs=wkv[hb][:, kc, 128:256],
                            start=(kc == 0), stop=(kc == 3)), reads=[b_wkv[hb], b_res], writes=[b_ps[bk]])
                if k4 % 2 == 0:
                    P.op('dve', lambda e, bk=bk, k4=k4, hb=hb: e.tensor_copy(
                        out=vsb[hb][:, k4 * 4:(k4 + 1) * 4, :].rearrange("p a b -> p (a b)"), in_=ps[bk][:, 0:512]),
                        reads=[b_ps[bk]], writes=[b_v[hb]])
                else:
                    P.op('act', lambda e, bk=bk, k4=k4, hb=hb: e.copy(
                        out=vsb[hb][:, k4 * 4:(k4 + 1) * 4, :].rearrange("p a b -> p (a b)"), in_=ps[bk][:, 0:512]),
                        reads=[b_ps[bk]], writes=[b_v[hb]])
            if h + 1 < nheads:
                load_head_w(h + 1)
            for qi, (q0, qw) in enumerate(QBS):
                bo = 4 + (qbi % 2)
                bl = 6 + (qbi % 2)
                ri = qbi % 2
                qbi += 1
                tiles = []
                if qi == 0:
                    tiles += [(kt, 0, None) for kt in range(0, 6)] + [(kt, 0, None) for kt in range(16, 32)]
                    tiles += [(6, 0, ('full', M6)), (7, 0, ('full', M7))]
                elif qi == 1:
                    tiles += [(kt, 0, None) for kt in range(0, 8)] + [(kt, 0, None) for kt in range(16, 32)]
                    tiles += [(8 + j, 128 * j, ('tri', TRI)) for j in range(4)]
                else:
                    tiles += [(kt, 0, None) for kt in range(0, 12)] + [(kt, 0, None) for kt in range(16, 32)]
                    tiles += [(12 + j, 128 * j, ('tri', TRI)) for j in range(4)]
                nt = len(tiles)
                LOOK = 3
                binfo = {}

                def emit_qk(ti):
                    kt, c0, mk = tiles[ti]
                    bk = rot4.next()
                    pi = prot.next()
                    binfo[ti] = (bk, pi)
                    P.op('pe', lambda e, bk=bk, kt=kt, c0=c0, q0=q0, qw=qw, hb=hb: e.matmul(
                        ps[bk][:, c0:qw], lhsT=knT[hb][:, kt * 128:(kt + 1) * 128], rhs=qn[hb][:, q0 + c0:q0 + qw],
                        start=True, stop=False), reads=[b_kn[hb], b_qn[hb]], writes=[b_ps[bk]])
                    P.op('pe', lambda e, bk=bk, kt=kt, c0=c0, q0=q0, qw=qw, hb=hb: e.matmul(
                        ps[bk][:, c0:qw], lhsT=krT[:, kt * 128:(kt + 1) * 128], rhs=qr[hb][:, q0 + c0:q0 + qw],
                        start=False, stop=True), reads=[b_res, b_qr[hb]], writes=[b_ps[bk]])
                for ti in range(min(LOOK, nt)):
                    emit_qk(ti)
                for ti, (kt, c0, mk) in enumerate(tiles):
                    if ti + LOOK < nt:
                        emit_qk(ti + LOOK)
                    bk, pi = binfo[ti]
                    P.op('act', lambda e, bk=bk, pi=pi, kt=kt, c0=c0, qw=qw: e.activation(
                        out=pT[pi][:, c0:qw], in_=ps[bk][:, c0:qw], func=AF.Exp, bias=kb0_sb[:, kt:kt + 1], scale=SCALE),
                        reads=[b_ps[bk], b_const], writes=[b_pT[pi]])
                    if mk is not None:
                        mw = qw if mk[0] == 'full' else 128
                        P.op('dve', lambda e, pi=pi, c0=c0, mw=mw, mk=mk: e.tensor_tensor(
                            out=pT[pi][:, c0:c0 + mw], in0=pT[pi][:, c0:c0 + mw], in1=mk[1][:, 0:mw], op=ALU.mult),
                            reads=[b_pT[pi], b_const], writes=[b_pT[pi]])
                    P.op('pe', lambda e, pi=pi, kt=kt, c0=c0, ti=ti, qw=qw, hb=hb, bo=bo, nt=nt: e.matmul(
                        ps[bo][:, c0:qw], lhsT=vsb[hb][:, kt, :], rhs=pT[pi][:, c0:qw],
                        start=(ti == 0), stop=(ti == nt - 1)), reads=[b_v[hb], b_pT[pi]], writes=[b_ps[bo]])
                    P.op('pe', lambda e, pi=pi, c0=c0, ti=ti, qw=qw, bl=bl, nt=nt: e.matmul(
                        ps[bl][:, c0:qw], lhsT=ones_bf, rhs=pT[pi][:, c0:qw],
                        start=(ti == 0), stop=(ti == nt - 1)), reads=[b_const, b_pT[pi]], writes=[b_ps[bl]])
                P.op('dve', lambda e, bl=bl, ri=ri, qw=qw: e.tensor_scalar(
                    out=rl[ri][:, 0:qw], in0=ps[bl][:, 0:qw], scalar1=1e-30, scalar2=None, op0=ALU.add),
                    reads=[b_ps[bl]], writes=[b_rl[ri]])
                P.op('dve', lambda e, ri=ri, qw=qw: e.reciprocal(out=rl[ri][:, 0:qw], in_=rl[ri][:, 0:qw]),
                     reads=[b_rl[ri]], writes=[b_rl[ri]])
                P.op('dve', lambda e, bo=bo, ri=ri, q0=q0, qw=qw, hb=hb: e.tensor_tensor(
                    out=aoh[hb][:, q0:q0 + qw], in0=ps[bo][:, 0:qw], in1=rl[ri][:, 0:qw], op=ALU.mult),
                    reads=[b_ps[bo], b_rl[ri]], writes=[b_aoh[hb]])
            ld('sp', AO[h * 128:(h + 1) * 128, :], aoh[hb], [b_d["AO"]], f'st{hb}', reads=[b_aoh[hb]])
        P.barrier()

    def phase_proj(wd, src, T, tbs, col0=0):
        C = carver()
        slots = mk_slots(C, 3, 1, KC, 128, "pj")
        xT, b_x = load_xT(C, src, T, col0)
        yst = [C.get([128, T], F32) for _ in range(2)]
        b_yst = [P.buf() for _ in range(2)]

        def epi(j, g, t0, w, bank):
            P.op('act', lambda e: e.copy(out=yst[j % 2][:, t0:t0 + w], in_=ps[bank][:, 0:w]),
                 reads=[b_ps[bank]], writes=[b_yst[j % 2]])

        def post(j):
            ld('sp', Y[j * 128:(j + 1) * 128, 0:T], yst[j % 2], [b_d["Y"]], f'st{j % 2}', reads=[b_yst[j % 2]])
        stream_gemm(wd, 32, 1, KC, 128, lambda kc, t0, w: xT[:, kc, t0:t0 + w], b_x, tbs, epi, post, rot8, slots, "pj")
        P.barrier()

    def phase_norm(T, tbs, hsrc, hcol0, g_post, hdst, outs, flagcols=0, final=False):
        C = carver()
        W = tbs[0][1]
        yb = C.get([128, KC, W], F32)
        b_yb = P.buf()
        hb = C.get([128, KC, W], F32)
        b_hb = P.buf()
        sq = C.get([128, KC, W], BF16)
        b_sq = P.buf()
        hnb = [C.get([128, KC, W], BF16) for _ in range(2)]
        b_hnb = [P.buf() for _ in range(2)]
        rstd = C.get([128, W], F32)
        b_rstd = P.buf()
        rstd2 = C.get([128, W], F32)
        b_rstd2 = P.buf()
        yv = Y.ap().rearrange("(k p) t -> p k t", p=128)
        is_x = (hsrc is None)
        hv = None if is_x else hsrc.ap().rearrange("(k p) t -> p k t", p=128)
        hsname = None if is_x else (hsrc.name if hsrc.name in b_d else None)
        oi = 0
        for (t0, w) in tbs:
            for q2 in range(2):
                ld('sp', yb[:, q2 * 16:(q2 + 1) * 16, 0:w], yv[:, q2 * 16:(q2 + 1) * 16, t0:t0 + w], [b_yb], f'x{q2}', reads=[b_d["Y"]])
                if not is_x:
                    ld('sp', hb[:, q2 * 16:(q2 + 1) * 16, 0:w], hv[:, q2 * 16:(q2 + 1) * 16, hcol0 + t0:hcol0 + t0 + w], [b_hb], f'x{2 + q2}',
                       reads=[b_d[hsname]] if hsname else [])
            if is_x:
                A_, B_ = hcol0 + t0, hcol0 + t0 + w
                for bi_, blk in enumerate(range(A_ // 128, (B_ - 1) // 128 + 1)):
                    lo, hi = max(A_, blk * 128), min(B_, (blk + 1) * 128)
                    ld('sp', hb[:, :, lo - A_:hi - A_],
                       xk[blk, :, :].rearrange("p (k w) -> p k w", k=KC)[:, :, lo - blk * 128:hi - blk * 128], [b_hb], f'x{2 + bi_ % 2}')
            bk = rot8.next()
            sumsq(yb[:, :, 0:w], b_yb, sq[:, :, 0:w], b_sq, KC, w, bk)
            rstd_from_ps(bk, 128, w, rstd[:, 0:w], b_rstd, 1.0 / D)
            for kc in range(KC):
                wr = [b_yb] if kc in (0, KC - 1) else []
                P.op('dve', lambda e, w=w, kc=kc: e.scalar_tensor_tensor(
                    out=yb[:, kc, 0:w], in0=yb[:, kc, 0:w], scalar=g_sb[:, g_post, kc:kc + 1], in1=rstd[:, 0:w],
                    op0=ALU.mult, op1=ALU.mult), reads=[b_yb, b_const, b_rstd], writes=wr)
            P.op('dve', lambda e, w=w: e.tensor_tensor(out=hb[:, :, 0:w], in0=hb[:, :, 0:w], in1=yb[:, :, 0:w], op=ALU.add),
                 reads=[b_yb, b_hb], writes=[b_hb])
            if flagcols > t0:
                fw_ = min(flagcols, t0 + w) - t0
                P.op('dve', lambda e, fw_=fw_: e.tensor_scalar(out=hb[:, :, 0:fw_], in0=hb[:, :, 0:fw_], scalar1=flag_sb[:, 0:1],
                                                               scalar2=None, op0=ALU.mult), reads=[b_hb, b_const], writes=[b_hb])
            if final:
                a, b = max(t0, OWN1), t0 + w
                ld('sp', out_d.ap().rearrange("(k p) t -> p k t", p=128)[:, :, a - OWN1:b - OWN1], hb[:, :, a - t0:b - t0],
                   [b_d["out"]], 'st0', reads=[b_hb])
                continue
            ld('sp', hdst.ap().rearrange("(k p) t -> p k t", p=128)[:, :, t0:t0 + w], hb[:, :, 0:w], [b_d[hdst.name]], 'st0', reads=[b_hb])
            bk = rot8.next()
            sumsq(hb[:, :, 0:w], b_hb, sq[:, :, 0:w], b_sq, KC, w, bk)
            rstd_from_ps(bk, 128, w, rstd2[:, 0:w], b_rstd2, 1.0 / D)
            for (gi, dst) in outs:
                o2 = oi % 2
                oi += 1
                for kc in range(KC):
                    wr = [b_hnb[o2]] if kc in (0, KC - 1) else []
                    P.op('dve', lambda e, w=w, gi=gi, kc=kc, o2=o2: e.scalar_tensor_tensor(
                        out=hnb[o2][:, kc, 0:w], in0=hb[:, kc, 0:w], scalar=g_sb[:, gi, kc:kc + 1], in1=rstd2[:, 0:w],
                        op0=ALU.mult, op1=ALU.mult), reads=[b_hb, b_const, b_rstd2], writes=wr)
                ld('sp', dst.ap().rearrange("(k p) t -> p k t", p=128)[:, :, t0:t0 + w], hnb[o2][:, :, 0:w], [b_d[dst.name]],
                   f'st{1 + o2}', reads=[b_hnb[o2]])
        P.barrier()

    def phase_F1(l, src, T, tbs, col0=0):
        C = carver()
        slots = mk_slots(C, 3, 2, KC, 128, "f1")
        xT, b_x = load_xT(C, src, T, col0)
        cw_sb = C.get([128, 3, 2 * NJ], F32)
        cb_sb = C.get([128, 2 * NJ], F32)
        b_c = P.buf()
        ld('sp', cw_sb, cw[l].ap(), [b_c], 'x4')
        ld('sp', cb_sb, cb[l].ap(), [b_c], 'x5')
        u = [[C.get([128, T], F32) for _ in range(2)] for _ in range(2)]
        b_u = [[P.buf() for _ in range(2)] for _ in range(2)]
        cv = [[C.get([128, T], F32) for _ in range(2)] for _ in range(2)]
        b_cv = [[P.buf() for _ in range(2)] for _ in range(2)]
        ao = [C.get([128, T], BF16) for _ in range(2)]
        b_ao = [P.buf() for _ in range(2)]
        for i in range(2):
            P.op('dve', lambda e, i=i: e.memset(ao[i][:, 0:2], 0.0), writes=[b_ao[i]])

        def epi(j, g, t0, w, bank):
            ub = j % 2
            P.op('act', lambda e: e.copy(out=u[g][ub][:, t0:t0 + w], in_=ps[bank][:, 0:w]), reads=[b_ps[bank]], writes=[b_u[g][ub]])
            if t0 + w == T:
                blk = g * NJ + j
                U, Cv = u[g][ub], cv[g][ub]
                eng = 'dve'
                P.op(eng, lambda e: e.tensor_scalar(out=Cv[:, 2:T], in0=U[:, 2:T], scalar1=cw_sb[:, 2, blk:blk + 1],
                                                    scalar2=cb_sb[:, blk:blk + 1], op0=ALU.mult, op1=ALU.add),
                     reads=[b_u[g][ub], b_c], writes=[b_cv[g][ub]])
                P.op(eng, lambda e: e.scalar_tensor_tensor(out=Cv[:, 2:T], in0=U[:, 1:T - 1], scalar=cw_sb[:, 1, blk:blk + 1],
                                                           in1=Cv[:, 2:T], op0=ALU.mult, op1=ALU.add),
                     reads=[b_u[g][ub], b_c, b_cv[g][ub]], writes=[b_cv[g][ub]])
                P.op(eng, lambda e: e.scalar_tensor_tensor(out=Cv[:, 2:T], in0=U[:, 0:T - 2], scalar=cw_sb[:, 0, blk:blk + 1],
                                                           in1=Cv[:, 2:T], op0=ALU.mult, op1=ALU.add),
                     reads=[b_u[g][ub], b_c, b_cv[g][ub]], writes=[b_cv[g][ub]])

        def post(j):
            ub = j % 2
            P.op('act', lambda e: e.activation(out=cv[0][ub][:, 2:T], in_=cv[0][ub][:, 2:T], func=AF.Silu),
                 reads=[b_cv[0][ub]], writes=[b_cv[0][ub]])
            P.op('dve', lambda e: e.tensor_tensor(out=ao[ub][:, 2:T], in0=cv[0][ub][:, 2:T], in1=cv[1][ub][:, 2:T], op=ALU.mult),
                 reads=[b_cv[0][ub], b_cv[1][ub]], writes=[b_ao[ub]])
            ld('sp', ACT[j, :, 0:T], ao[ub], [b_d["ACT"]], f'st{ub}', reads=[b_ao[ub]])
        stream_gemm(w_in[l], NJ, 2, KC, 128, lambda kc, t0, w: xT[:, kc, t0:t0 + w], b_x, tbs, epi, post, rot8, slots, "f1")
        P.barrier()

    def phase_F2(l, T, tbs):
        W = tbs[0][1]
        for half in range(2):
            C = carver()
            slots = mk_slots(C, 3, 1, NJ, 128, "f2")
            hT = 2 * W
            h0 = half * hT
            aT = C.get([128, NJ, hT], BF16)
            b_a = P.buf()
            av = ACT.ap().rearrange("j p t -> p j t")
            nq = 6
            for q in range(nq):
                j0, j1 = q * NJ // nq, (q + 1) * NJ // nq
                ld('sp', aT[:, j0:j1, :], av[:, j0:j1, h0:h0 + hT], [b_a], f'x{q}', reads=[b_d["ACT"]])
            yst = [C.get([128, hT], F32) for _ in range(2)]
            b_yst = [P.buf() for _ in range(2)]

            def epi(j, g, t0, w, bank, yst=yst, b_yst=b_yst):
                P.op('act', lambda e: e.copy(out=yst[j % 2][:, t0:t0 + w], in_=ps[bank][:, 0:w]),
                     reads=[b_ps[bank]], writes=[b_yst[j % 2]])

            def post(j, yst=yst, b_yst=b_yst, h0=h0, hT=hT):
                ld('sp', Y[j * 128:(j + 1) * 128, h0:h0 + hT], yst[j % 2], [b_d["Y"]], f'st{j % 2}', reads=[b_yst[j % 2]])
            stream_gemm(w_out[l], 32, 1, NJ, 128, lambda kc, t0, w, aT=aT: aT[:, kc, t0:t0 + w], b_a,
                        [(0, W), (W, W)], epi, post, rot8, slots, "f2")
            P.barrier()

    def phase_B1():
        C = carver()
        slots = mk_slots(C, 3, 1, KC, 128, "b1")
        xT, b_x = load_xT(C, HKV, T0)
        wv = C.get([128, KC, 512], BF16)
        b_wv = P.buf()
        for q4 in range(4):
            if variant in (3, 4):
                continue
            P.op('pool', lambda e, q4=q4: e.dma_start(out=wv[:, q4 * 8:(q4 + 1) * 8, :].rearrange("p k m -> p (k m)"),
                                                      in_=w_v.ap()[:, q4 * 4096:(q4 + 1) * 4096]), writes=[b_wv], dma='w3')
        kst = [C.get([128, T0], BF16) for _ in range(2)]
        b_kst = [P.buf() for _ in range(2)]

        def epi(j, g, t0, w, bank):
            P.op('act', lambda e: e.copy(out=kst[j % 2][:, t0:t0 + w], in_=ps[bank][:, 0:w]), reads=[b_ps[bank]], writes=[b_kst[j % 2]])

        def post(j):
            ld('sp', KD[j, :, :], kst[j % 2], [b_d["KD"]], f'st{j % 2}', reads=[b_kst[j % 2]])
        stream_gemm(w_k, 8, 1, KC, 128, lambda kc, t0, w: xT[:, kc, t0:t0 + w], b_x, TB0, epi, post, rot8, slots, "b1")
        vl = [C.get([128, 8, 128], BF16) for _ in range(2)]
        vr = [C.get([128, 8, 128], BF16) for _ in range(2)]
        b_vl = [P.buf() for _ in range(2)]
        b_vr = [P.buf() for _ in range(2)]
        for i in range(2):
            if variant in (3, 5):
                continue
            P.op('dve', lambda e, i=i: e.memset(vl[i], 0.0), writes=[b_vl[i]])
            P.op('dve', lambda e, i=i: e.memset(vr[i], 0.0), writes=[b_vr[i]])
        for tt in range(10):
            if variant in (1, 3, 4, 5) or (variant == 2 and tt == 9):
                continue
            k0 = tt * 128
            nk = min(128, T0 - k0)
            bk = rot8.next()
            i2 = tt % 2
            for kc in range(KC):
                P.op('pe', lambda e, bk=bk, kc=kc, k0=k0, nk=nk: e.matmul(
                    ps[bk][0:nk, 0:512], lhsT=xT[:, kc, k0:k0 + nk], rhs=wv[:, kc, :],
                    start=(kc == 0), stop=(kc == KC - 1)), reads=[b_x, b_wv], writes=[b_ps[bk]])
            P.op('act', lambda e, bk=bk, nk=nk, i2=i2: e.copy(
                out=vl[i2][0:nk, :, 0:64], in_=ps[bk][0:nk, 0:512].rearrange("p (g d) -> p g d", g=8)),
                reads=[b_ps[bk]], writes=[b_vl[i2]])
            P.op('dve', lambda e, nk=nk, i2=i2: e.tensor_copy(out=vr[i2][0:nk, :, 64:128], in_=vl[i2][0:nk, :, 0:64]),
                 reads=[b_vl[i2]], writes=[b_vr[i2]])
            if variant == 6:
                continue
            ld('sp', VL[tt, :, :], vl[i2].rearrange("p g d -> p (g d)"), [b_d["VL"]], f'st{2 + i2}', reads=[b_vl[i2]])
            ld('sp', VR[tt, :, :], vr[i2].rearrange("p g d -> p (g d)"), [b_d["VR"]], f'st{4 + i2}', reads=[b_vr[i2]])
        P.barrier()

    def phase_B2(npairs=32):
        C = carver()
        slots = mk_slots(C, 3, 1, KC, 128, "b2")
        xT, b_x = load_xT(C, HNQ, T1, 128)
        kT = C.get([128, 8, T0], BF16)
        vL = C.get([128, 10, 8, 128], BF16)
        vR = C.get([128, 10, 8, 128], BF16)
        b_kv = P.buf()
        ld('sp', kT, KD.ap().rearrange("g p t -> p g t"), [b_kv], 'x4', reads=[b_d["KD"]])
        ld('sp', vL.rearrange("p t g d -> p t (g d)"), VL.ap().rearrange("t p f -> p t f"), [b_kv], 'x5', reads=[b_d["VL"]])
        ld('sp', vR.rearrange("p t g d -> p t (g d)"), VR.ap().rearrange("t p f -> p t f"), [b_kv], 'x6', reads=[b_d["VR"]])
        q2 = [C.get([128, T1], BF16) for _ in range(2)]
        b_q2 = [P.buf() for _ in range(2)]
        ehs = [C.get([128, 2, 256], F32) for _ in range(2)]
        b_eh = [P.buf() for _ in range(2)]
        pex = [C.get([128, 256], F32) for _ in range(4)]
        b_pex = [P.buf() for _ in range(4)]
        pT = [C.get([128, 256], BF16) for _ in range(4)]
        b_pT = [P.buf() for _ in range(4)]
        ao2 = [C.get([128, T1], BF16) for _ in range(2)]
        b_ao2 = [P.buf() for _ in range(2)]
        rl = [C.get([128, 128], F32) for _ in range(2)]
        b_rl = [P.buf() for _ in range(2)]
        rot4 = Rot(range(4))
        SC = 64 ** -0.5
        cnt = [0, 0]

        def epi(j, g, t0, w, bank):
            P.op('act', lambda e: e.copy(out=q2[j % 2][:, t0:t0 + w], in_=ps[bank][:, 0:w]), reads=[b_ps[bank]], writes=[b_q2[j % 2]])

        def post(m):
            m2 = m % 2
            ld('sp', ehs[m2], eh[m, :, :, :], [b_eh[m2]], f'x{7 + m2}')
            g = (2 * m) // 8
            units = [(qt, hh) for qt in range(9) for hh in range(2)]
            info = {}
            acc = {}

            def geom(qt):
                qa = 128 + qt * 128
                nq = min(128, T0 - qa)
                return qa, nq, nq

            def stA(u):
                qt, hh = units[u]
                qa, nq, nkc = geom(qt)
                pb = 64 * hh
                bk = rot4.next()
                pi = cnt[1] % 4
                cnt[1] += 1
                info[u] = (bk, pi)
                P.op('pe', lambda e: e.matmul(
                    ps[bk][:, 0:nq], lhsT=kT[pb:pb + 64, g, qa - 128:qa], rhs=q2[m2][pb:pb + 64, qa - 128:qa - 128 + nq],
                    start=True, stop=True), reads=[b_kv, b_q2[m2]], writes=[b_ps[bk]])
                P.op('pe', lambda e: e.matmul(
                    ps[bk][0:nkc, 128:128 + nq], lhsT=kT[pb:pb + 64, g, qa:qa + nkc], rhs=q2[m2][pb:pb + 64, qa - 128:qa - 128 + nq],
                    start=True, stop=True), reads=[b_kv, b_q2[m2]], writes=[b_ps[bk]])

            def stB(u):
                qt, hh = units[u]
                qa, nq, nkc = geom(qt)
                bk, pi = info[u]
                kprev = qt
                P.op('act', lambda e: e.activation(
                    out=pex[pi][:, 0:nq], in_=ps[bk][:, 0:nq], func=AF.Exp, bias=kb1_sb[:, kprev:kprev + 1], scale=SC),
                    reads=[b_ps[bk], b_const], writes=[b_pex[pi]])
                P.op('act', lambda e: e.activation(
                    out=pex[pi][0:nkc, 128:128 + nq], in_=ps[bk][0:nkc, 128:128 + nq], func=AF.Exp,
                    bias=kb1_sb[0:nkc, kprev + 1:kprev + 2], scale=SC),
                    reads=[b_ps[bk], b_const], writes=[b_pex[pi]])
                P.op('dve', lambda e: e.tensor_tensor(
                    out=pT[pi][:, 0:nq], in0=pex[pi][:, 0:nq], in1=ehs[m2][:, hh, 0:nq], op=ALU.mult),
                    reads=[b_pex[pi], b_eh[m2]], writes=[b_pT[pi]])
                P.op('dve', lambda e: e.tensor_tensor(
                    out=pT[pi][0:nkc, 128:128 + nq], in0=pex[pi][0:nkc, 128:128 + nq], in1=ehs[m2][0:nkc, hh, 128:128 + nq], op=ALU.mult),
                    reads=[b_pex[pi], b_eh[m2]], writes=[b_pT[pi]])

            def stC(u):
                qt, hh = units[u]
                qa, nq, nkc = geom(qt)
                bk, pi = info[u]
                kprev = qt
                if hh == 0:
                    acc[qt] = (4 + (cnt[0] % 2), 6 + (cnt[0] % 2), cnt[0] % 2)
                    cnt[0] += 1
                bo, bl, ri = acc[qt]
                vv = vL if hh == 0 else vR
                on = onesL if hh == 0 else onesR
                for ii, (kti, c0, nkk) in enumerate(((kprev, 0, 128), (kprev + 1, 128, nkc))):
                    nmm = 2 * hh + ii
                    P.op('pe', lambda e, kti=kti, c0=c0, nkk=nkk, nmm=nmm: e.matmul(
                        ps[bo][:, 0:nq], lhsT=vv[0:nkk, kti, g, :], rhs=pT[pi][0:nkk, c0:c0 + nq],
                        start=(nmm == 0), stop=(nmm == 3)), reads=[b_kv, b_pT[pi]], writes=[b_ps[bo]])
                    P.op('pe', lambda e, c0=c0, nkk=nkk, nmm=nmm: e.matmul(
                        ps[bl][:, 0:nq], lhsT=on[0:nkk, :], rhs=pT[pi][0:nkk, c0:c0 + nq],
                        start=(nmm == 0), stop=(nmm == 3)), reads=[b_const, b_pT[pi]], writes=[b_ps[bl]])
                if hh == 1:
                    P.op('dve', lambda e: e.tensor_scalar(
                        out=rl[ri][:, 0:nq], in0=ps[bl][:, 0:nq], scalar1=esink[:, m:m + 1], scalar2=None, op0=ALU.add),
                        reads=[b_ps[bl], b_const], writes=[b_rl[ri]])
                    P.op('dve', lambda e: e.reciprocal(out=rl[ri][:, 0:nq], in_=rl[ri][:, 0:nq]),
                         reads=[b_rl[ri]], writes=[b_rl[ri]])
                    P.op('dve', lambda e: e.tensor_tensor(
                        out=ao2[m2][:, qa - 128:qa - 128 + nq], in0=ps[bo][:, 0:nq], in1=rl[ri][:, 0:nq], op=ALU.mult),
                        reads=[b_ps[bo], b_rl[ri]], writes=[b_ao2[m2]])
            LOOK = 2
            nu = len(units)
            for u in range(min(LOOK, nu)):
                stA(u)
            for u in range(nu):
                if u + LOOK < nu:
                    stA(u + LOOK)
                stB(u)
                stC(u)
            ld('sp', AO[m * 128:(m + 1) * 128, 0:T1], ao2[m2], [b_d["AO"]], f'st{m2}', reads=[b_ao2[m2]])
        stream_gemm(w_q, npairs, 1, KC, 128, lambda kc, t0, w: xT[:, kc, t0:t0 + w], b_x, TB1, epi, post, rot4, slots, "b2")
        P.barrier()

    class XE:
        name = "xk"
        @staticmethod
        def ap():
            return xk.ap()
    phases = [
        ("A1", lambda: phase_A1()),
        ("A2", lambda: phase_A2()),
        ("A3", lambda: phase_A3()),
        ("A4", lambda: phase_proj(w_o0, AO, T0, TB0)),
        ("A5", lambda: phase_norm(T0, TB0, None, EXT0, 1, HM0, [(2, HN)])),
        ("F1a", lambda: phase_F1(0, HN, T0, TB0)),
        ("F2a", lambda: phase_F2(0, T0, TB0)),
        ("F3a", lambda: phase_norm(T0, TB0, HM0, 0, 3, H1, [(4, HNQ), (8, HKV)], flagcols=HALO)),
        ("B1", lambda: phase_B1()),
        ("B2", lambda: phase_B2()),
        ("B3", lambda: phase_proj(w_o1, AO, T1, TB1)),
        ("B4", lambda: phase_norm(T1, TB1, H1, 128, 5, HM1, [(6, HN)])),
        ("F1b", lambda: phase_F1(1, HN, T1, TB1)),
        ("F2b", lambda: phase_F2(1, T1, TB1)),
        ("F3b", lambda: phase_norm(T1, TB1, HM1, 0, 7, None, [], final=True)),
    ]
    for name, fn in phases:
        if name in run_list:
            fn()
    P.finish('sp')
    P.emit()
    return nc


def tile_w(W, mcols):
    Kd, N = W.shape
    return np.ascontiguousarray(
        W.reshape(Kd // 128, 128, N // mcols, mcols).transpose(2, 1, 0, 3)).reshape(N // mcols, 128, (Kd // 128) * mcols)


def gt(g):
    return np.ascontiguousarray(g.reshape(-1, 128).T)


def prep_shared(inp):
    import ml_dtypes
    f = np.float32
    sh = {}
    gains = np.stack([gt(inp["norm_mix_pre"][0]), gt(inp["norm_mix_post"][0]), gt(inp["norm_ffn_pre"][0]),
                      gt(inp["norm_ffn_post"][0]), gt(inp["norm_mix_pre"][1]), gt(inp["norm_mix_post"][1]),
                      gt(inp["norm_ffn_pre"][1]), gt(inp["norm_ffn_post"][1]), gt(inp["shared_kv_norm"])], axis=1)
    sh["gains"] = np.ascontiguousarray(gains.astype(f))
    sh["qn_g"] = gt(inp["mla_q_norm"][0]).astype(f)
    sh["kvn_g"] = gt(inp["mla_kv_norm"][0]).astype(f)
    wdkv = inp["mla_w_dkv"][0]
    wdkv_e = np.concatenate([wdkv, wdkv[:, 544:576], wdkv[:, 512:544]], axis=1)
    sh["w_dkv"] = tile_w(wdkv_e, 640)[0]
    sh["w_dq"] = tile_w(inp["mla_w_dq"][0], 128)
    wuq = inp["mla_w_uq"][0].reshape(1024, 32, 192)
    wuq_e = np.concatenate([wuq, wuq[:, :, 160:192], wuq[:, :, 128:160]], axis=2).reshape(1024, 32 * 256)
    sh["w_uq"] = tile_w(wuq_e, 256)
    sh["w_ukv"] = tile_w(inp["mla_w_ukv"][0], 256)
    sh["w_o0"] = tile_w(inp["mla_w_o"][0], 128)
    wk = inp["swa_w_k"].reshape(D, 8, 64)
    sh["w_k"] = tile_w(np.concatenate([wk, wk], axis=2).reshape(D, 1024), 128)
    sh["w_v"] = tile_w(inp["swa_w_v"], 512)[0]
    sh["w_q"] = tile_w(inp["swa_w_q"][0], 128)
    sh["w_o1"] = tile_w(inp["swa_w_o"][0], 128)
    for l in range(2):
        wi = inp["ffn_w_in"][l].reshape(KC, 128, 2, NJ, 128)
        sh[f"w_in{l}"] = np.ascontiguousarray(wi.transpose(3, 1, 2, 0, 4)).reshape(NJ, 128, 2 * KC * 128)
        sh[f"w_out{l}"] = tile_w(inp["ffn_w_out"][l], 128)
        sh[f"cw{l}"] = np.ascontiguousarray(inp["ffn_conv_w"][l].reshape(3, 2 * NJ, 128).transpose(2, 0, 1)).astype(f)
        sh[f"cb{l}"] = np.ascontiguousarray(inp["ffn_conv_b"][l].reshape(2 * NJ, 128).T).astype(f)
    s = inp["swa_sinks"][0]
    sk = np.zeros((128, 32), f)
    sk[0:64, :] = s[0::2][None, :]
    sk[64:128, :] = s[1::2][None, :]
    sh["sinks"] = sk
    p = np.arange(64)
    rc = np.zeros((64, 4), f)
    rc[:, 0] = ((10000.0 ** (-(p % 32).astype(np.float64) / 32)) / (2.0 * math.pi)).astype(f)
    rc[:, 1] = 0.25
    rc[:, 2] = np.where(p < 32, 0.5, 0.0)
    rc[:, 3] = 0.0
    sh["ropec"] = rc
    kk = np.arange(128)[:, None]
    tri = (kk <= np.arange(128)[None, :]).astype(f)
    col = np.arange(HALO)[None, :]
    m6 = ((768 + kk) <= (EXT0 + col)).astype(f)
    m7 = ((896 + kk) <= (EXT0 + col)).astype(f)
    sh["masks"] = np.concatenate([tri, m6, m7], axis=1).astype(ml_dtypes.bfloat16)
    slopes = 2.0 ** (-8.0 * np.arange(1, 65, dtype=np.float64) / 64)
    ehv = np.zeros((32, 128, 2, 256), f)
    cc = np.arange(128)[None, :]
    for h in range(64):
        dprev = 128 + cc - kk
        dcur = cc - kk
        ehv[h // 2, :, h % 2, 0:128] = np.where(dprev <= 127, np.exp(-slopes[h] * dprev), 0.0)
        ehv[h // 2, :, h % 2, 128:256] = np.where(dcur >= 0, np.exp(-slopes[h] * dcur), 0.0)
    sh["eh"] = ehv
    return sh


def prep_core(inp, b, c):
    f = np.float32
    x = inp["x"][b]
    posv = inp["positions"][b]
    xkv = np.zeros((D, NK), f)
    pp = np.zeros((NK,), np.int32)
    kb0 = np.zeros((128, 32), f)
    for grp, ch in enumerate([c - 1, c, c - 2, c - 3]):
        if ch >= 0:
            xkv[:, grp * CH:(grp + 1) * CH] = x[ch * CH:(ch + 1) * CH].T
            pp[grp * CH:(grp + 1) * CH] = posv[ch * CH:(ch + 1) * CH]
        else:
            kb0[:, grp * 8:(grp + 1) * 8] = NEG
    kb1 = np.zeros((128, 10), f)
    if c == 0:
        for tt in range(10):
            colv = tt * 128 + np.arange(128)
            kb1[:, tt] = np.where(colv < HALO, NEG, 0.0)
    fl = np.full((128, 1), 1.0 if c > 0 else 0.0, f)
    xkt = np.ascontiguousarray(xkv.reshape(KC, 128, NK // 128, 128).transpose(2, 1, 0, 3)).reshape(NK // 128, 128, KC * 128)
    return {"xk": xkt, "pos": pp, "kbias0": kb0, "kbias1": kb1, "flag": fl}


_CACHE = {}


def kernel(**inputs):
    inp = {k_: np.asarray(v) for k_, v in inputs.items()}
    sh = prep_shared(inp)
    if "nc" not in _CACHE:
        _CACHE["nc"] = build()
    nc = _CACHE["nc"]
    in_maps = []
    for core in range(8):
        b, c = core // 4, core % 4
        m = dict(sh)
        m.update(prep_core(inp, b, c))
        in_maps.append(m)
    res = run_bass_kernel_spmd(nc, in_maps, core_ids=list(range(8)))
    out = np.zeros((2, SEQ, D), np.float32)
    for core in range(8):
        b, c = core // 4, core % 4
        out[b, c * CH:(c + 1) * CH, :] = res.results[core]["out"].T
    return out
```

```python
import contextlib
import math
import numpy as np
import concourse.bass as bass
import concourse.mybir as mybir
from concourse.bass_utils import run_bass_kernel_spmd

F32 = mybir.dt.float32
BF16 = mybir.dt.bfloat16
I32 = mybir.dt.int32
ALU = mybir.AluOpType
AF = mybir.ActivationFunctionType

ENGS = ['pe', 'act', 'dve', 'pool', 'sp']

D = 4096
KC = 32
SEQ = 4096
CH = 1024
HALO = 132
T0 = CH + HALO
T1 = T0 - 128
OWN1 = HALO - 128
NK = 4096
EXT0 = CH - HALO
DFF = 11008
NJ = DFF // 128
EPS = 1e-6
TB0 = [(i * 289, 289) for i in range(4)]
TB1 = [(i * 257, 257) for i in range(4)]
NEG = -30000.0


class Buf:
    __slots__ = ('name', 'w', 'rs', 'excl')

    def __init__(self, name):
        self.name = name
        self.w = None
        self.rs = {}
        self.excl = False


class Rot:
    def __init__(self, items):
        self.items = list(items)
        self.i = 0

    def next(self):
        v = self.items[self.i % len(self.items)]
        self.i += 1
        return v


class Prog:
    def __init__(self, nc):
        self.nc = nc
        self.ops = {e: [] for e in ENGS}
        self.chan_cnt = {}
        self.nbuf = 0

    def buf(self, name=None):
        self.nbuf += 1
        return Buf(name or f"b{self.nbuf}")

    def op(self, eng, fn, reads=(), writes=(), dma=None):
        idx = len(self.ops[eng])
        deps = {}

        def add(ev, same_ok):
            if ev is None:
                return
            k, i = ev
            if k == eng and same_ok:
                return
            if deps.get(k, -1) < i:
                deps[k] = i
        for b in reads:
            add(b.w, False)
            if b.excl:
                for k, i in b.rs.items():
                    add((k, i), True)
        for b in writes:
            add(b.w, True)
            for k, i in b.rs.items():
                add((k, i), True)
        if dma is not None:
            ck = 'ch:' + dma
            n = self.chan_cnt.get(dma, 0)
            if n > 0:
                add((ck, n - 1), False)
            self.chan_cnt[dma] = n + 1
            ev = (ck, n)
        else:
            ev = (eng, idx)
        self.ops[eng].append(dict(fn=fn, deps=deps, dma=dma, need=False))
        ws = set(id(b) for b in writes)
        for b in reads:
            if id(b) not in ws:
                if b.rs.get(ev[0], -1) < ev[1]:
                    b.rs[ev[0]] = ev[1]
        for b in writes:
            b.w = ev
            b.rs = {}
        return ev

    def wait_events(self, eng, evs):
        deps = {}
        for k, i in evs:
            if deps.get(k, -1) < i:
                deps[k] = i
        self.ops[eng].append(dict(fn=None, deps=deps, dma=None, need=False))

    def _last_events(self):
        evs = []
        for ch, n in self.chan_cnt.items():
            evs.append(('ch:' + ch, n - 1))
        for e in ENGS:
            for i in range(len(self.ops[e]) - 1, -1, -1):
                o = self.ops[e][i]
                if o['dma'] is None and o['fn'] is not None:
                    evs.append((e, i))
                    break
        return evs

    def barrier(self):
        evs = self._last_events()
        for e in ENGS:
            self.wait_events(e, [ev for ev in evs if ev[0] != e])

    def finish(self, eng='sp'):
        self.wait_events(eng, [ev for ev in self._last_events() if ev[0] != eng])

    def emit(self):
        nc = self.nc
        for e in ENGS:
            for o in self.ops[e]:
                for k, i in o['deps'].items():
                    if not k.startswith('ch:'):
                        self.ops[k][i]['need'] = True
        val = {}
        for e in ENGS:
            c = 0
            v = []
            for o in self.ops[e]:
                if o['need']:
                    c += 1
                v.append(c)
            val[e] = v
        with contextlib.ExitStack() as st:
            sems = {}
            for e in ENGS:
                sems[e] = st.enter_context(nc.semaphore("s_" + e))
            for ch in self.chan_cnt:
                sems['ch:' + ch] = st.enter_context(nc.semaphore("c_" + ch))
            block = st.enter_context(nc.Block())
            engmap = {'pe': block.tensor, 'act': block.scalar, 'dve': block.vector,
                      'pool': block.gpsimd, 'sp': block.sync}

            def make(e):
                def body(engine):
                    waited = {}
                    for o in self.ops[e]:
                        for k, i in o['deps'].items():
                            v = 16 * (i + 1) if k.startswith('ch:') else val[k][i]
                            if waited.get(k, 0) < v:
                                engine.wait_ge(sems[k], v)
                                waited[k] = v
                        if o['fn'] is None:
                            continue
                        ins = o['fn'](engine)
                        if o['dma'] is not None:
                            ins.then_inc(sems['ch:' + o['dma']], 16)
                        elif o['need']:
                            ins.then_inc(sems[e], 1)
                return body
            for e in ENGS:
                if self.ops[e]:
                    engmap[e](make(e))


class K:
    pass


def build(dbg=(), stop_after=None, only=None, variant=0):
    nc = bass.Bass("TRN2", target_bir_lowering=False)
    P = Prog(nc)
    k = K()
    k.nc, k.P = nc, P

    def din(name, shape, dt=F32):
        return nc.dram_tensor(name, list(shape), dt, kind="ExternalInput")

    def dscr(name, shape, dt):
        kind = "ExternalOutput" if name in dbg else "Internal"
        return nc.dram_tensor(name, list(shape), dt, kind=kind)

    xk = din("xk", [NK // 128, 128, KC * 128])
    pos = din("pos", [NK], I32)
    kbias0 = din("kbias0", [128, 32])
    kbias1 = din("kbias1", [128, 10])
    flag = din("flag", [128, 1])
    ropec = din("ropec", [64, 4])
    masks = din("masks", [128, 128 + 2 * HALO], BF16)
    eh = din("eh", [32, 128, 2, 256])
    sinks = din("sinks", [128, 32])
    gains = din("gains", [128, 9, 32])
    qn_g = din("qn_g", [128, 8])
    kvn_g = din("kvn_g", [128, 4])
    w_dkv = din("w_dkv", [128, KC * 640])
    w_dq = din("w_dq", [8, 128, KC * 128])
    w_uq = din("w_uq", [32, 128, 8 * 256])
    w_ukv = din("w_ukv", [32, 128, 4 * 256])
    w_o0 = din("w_o0", [32, 128, KC * 128])
    w_k = din("w_k", [8, 128, KC * 128])
    w_v = din("w_v", [128, KC * 512])
    w_q = din("w_q", [32, 128, KC * 128])
    w_o1 = din("w_o1", [32, 128, KC * 128])
    PH = ["A1", "A2", "A3", "A4", "A5", "F1a", "F2a", "F3a", "B1", "B2", "B3", "B4", "F1b", "F2b", "F3b"]
    last = PH.index(stop_after) if stop_after else len(PH) - 1
    run_list = list(only) if only else PH[:last + 1]
    need_ffn = [any(p_ in run_list for p_ in (("F1a", "F2a"), ("F1b", "F2b"))[l]) for l in range(2)]
    w_in = [din(f"w_in{l}", [NJ, 128, 2 * KC * 128]) if need_ffn[l] else None for l in range(2)]
    w_out = [din(f"w_out{l}", [32, 128, NJ * 128]) if need_ffn[l] else None for l in range(2)]
    cw = [din(f"cw{l}", [128, 3, 2 * NJ]) for l in range(2)]
    cb = [din(f"cb{l}", [128, 2 * NJ]) for l in range(2)]
    out_d = nc.dram_tensor("out", [D, CH], F32, kind="ExternalOutput")

    HN0 = dscr("HN0", [D, T0], BF16)
    CT = dscr("CT", [512, NK], BF16)
    KR = dscr("KR", [64, NK], BF16)
    CSQ = dscr("CSQ", [2, 64, T0], F32)
    CQ = dscr("CQ", [1024, T0], BF16)
    AO = dscr("AO", [D, T0], BF16)
    Y = dscr("Y", [D, T0], F32)
    HM0 = dscr("HM0", [D, T0], F32)
    HN = dscr("HN", [D, T0], BF16)
    ACT = dscr("ACT", [NJ, 128, T0], BF16)
    H1 = dscr("H1", [D, T0], F32)
    HNQ = dscr("HNQ", [D, T0], BF16)
    HKV = dscr("HKV", [D, T0], BF16)
    KD = dscr("KD", [8, 128, T0], BF16)
    VL = dscr("VL", [10, 128, 8 * 128], BF16)
    VR = dscr("VR", [10, 128, 8 * 128], BF16)
    HM1 = dscr("HM1", [D, T1], F32)
    b_d = {n: P.buf("d_" + n) for n in ["HN0", "CT", "KR", "CSQ", "CQ", "AO", "Y", "HM0", "HN", "ACT", "H1",
                                       "HNQ", "HKV", "KD", "VL", "VR", "HM1", "out"]}

    ARENA_W = 51200
    arena = nc.alloc_sbuf_tensor("arena", [128, ARENA_W], F32)
    ps = [nc.alloc_psum_tensor(f"ps{i}", [128, 512], F32) for i in range(8)]
    b_ps = [P.buf(f"ps{i}") for i in range(8)]
    for b_ in b_ps:
        b_.excl = True

    class Carver:
        def __init__(self):
            self.off = 0

        def get(self, shape, dt, parts=128):
            n = int(np.prod(shape[1:]))
            nb = n * (4 if dt in (F32, I32) else 2)
            nw = (nb + 3) // 4
            nw = (nw + 7) // 8 * 8
            assert self.off + nw <= ARENA_W, (self.off, nw, shape)
            v = arena[0:shape[0], self.off:self.off + nw]
            self.off += nw
            if dt != F32:
                v = v.bitcast(dt)
            v = v[:, 0:n]
            if len(shape) == 3:
                v = v.rearrange("p (a b) -> p a b", a=shape[1])
            elif len(shape) == 4:
                v = v.rearrange("p (a b c) -> p a b c", a=shape[1], b=shape[2])
            return v

    CV = Carver()
    ones_bf = CV.get([128, 128], BF16)
    ones_f = CV.get([128, 128], F32)
    onesL = CV.get([128, 128], BF16)
    onesR = CV.get([128, 128], BF16)
    g_sb = CV.get([128, 9, 32], F32)
    qn_sb = CV.get([128, 8], F32)
    kvn_sb = CV.get([128, 4], F32)
    flag_sb = CV.get([128, 1], F32)
    kb0_sb = CV.get([128, 32], F32)
    kb1_sb = CV.get([128, 10], F32)
    mask_sb = CV.get([128, 128 + 2 * HALO], BF16)
    esink = CV.get([128, 32], F32)
    ropec_sb = CV.get([64, 4], F32)
    eps_sb = CV.get([128, 1], F32)
    b_const = P.buf("const")
    BASE = CV.off

    def carver():
        c = Carver()
        c.off = BASE
        return c

    def ld(eng, out, in_, writes, ch, reads=()):
        P.op(eng, lambda e: e.dma_start(out=out, in_=in_), reads=list(reads), writes=list(writes), dma=ch)

    ld('sp', g_sb, gains.ap(), [b_const], 'c0')
    ld('sp', qn_sb, qn_g.ap(), [b_const], 'c0')
    ld('sp', kvn_sb, kvn_g.ap(), [b_const], 'c0')
    ld('sp', flag_sb, flag.ap(), [b_const], 'c0')
    ld('sp', kb0_sb, kbias0.ap(), [b_const], 'c0')
    ld('sp', kb1_sb, kbias1.ap(), [b_const], 'c0')
    ld('sp', mask_sb, masks.ap(), [b_const], 'c0')
    ld('sp', esink, sinks.ap(), [b_const], 'c0')
    ld('sp', ropec_sb, ropec.ap(), [b_const], 'c0')
    P.op('dve', lambda e: e.memset(ones_bf, 1.0), writes=[b_const])
    P.op('dve', lambda e: e.memset(ones_f, 1.0), writes=[b_const])
    P.op('dve', lambda e: e.memset(eps_sb, EPS), writes=[b_const])
    P.op('dve', lambda e: e.memset(onesL, 0.0), writes=[b_const])
    P.op('dve', lambda e: e.memset(onesR, 0.0), writes=[b_const])
    P.op('dve', lambda e: e.memset(onesL[:, 0:64], 1.0), writes=[b_const])
    P.op('dve', lambda e: e.memset(onesR[:, 64:128], 1.0), writes=[b_const])
    P.op('act', lambda e: e.activation(out=esink, in_=esink, func=AF.Exp), reads=[b_const], writes=[b_const])
    P.barrier()

    def rstd_from_ps(bank, np_, w, out, b_out, inv_n):
        P.op('act', lambda e: e.activation(out=out, in_=ps[bank][0:np_, 0:w], func=AF.Sqrt, bias=eps_sb[0:np_, 0:1], scale=inv_n),
             reads=[b_ps[bank], b_const], writes=[b_out])
        P.op('dve', lambda e: e.reciprocal(out=out, in_=out), reads=[b_out], writes=[b_out])

    def sumsq(src3, b_src, sq3, b_sq, nch, w, bank):
        P.op('act', lambda e: e.activation(out=sq3, in_=src3, func=AF.Square), reads=[b_src], writes=[b_sq])
        for c_ in range(nch):
            P.op('pe', lambda e, c_=c_: e.matmul(ps[bank][:, 0:w], lhsT=ones_bf, rhs=sq3[:, c_, :],
                                                 start=(c_ == 0), stop=(c_ == nch - 1)),
                 reads=[b_sq, b_const], writes=[b_ps[bank]])

    def stream_gemm(wd, nblk, groups, kc_n, M, rhs_fn, b_x, tbs, epi, post, rot, slots, tag):
        for j in range(nblk):
            wv, b_w, ch = slots.next()
            P.op('pool', lambda e, j=j, wv=wv: e.dma_start(
                out=wv.rearrange("p g k m -> p (g k m)"), in_=wd[j, :, :]), writes=[b_w], dma=ch)
            for g in range(groups):
                for (t0, w) in tbs:
                    bank = rot.next()
                    for kc in range(kc_n):
                        P.op('pe', lambda e, bank=bank, wv=wv, g=g, kc=kc, t0=t0, w=w: e.matmul(
                            ps[bank][0:M, 0:w], lhsT=wv[:, g, kc, :], rhs=rhs_fn(kc, t0, w),
                            start=(kc == 0), stop=(kc == kc_n - 1)),
                            reads=[b_w, b_x], writes=[b_ps[bank]])
                    epi(j, g, t0, w, bank)
            if post is not None:
                post(j)

    def mk_slots(C, n, groups, kc_n, M, tag):
        return Rot([(C.get([128, groups, kc_n, M], BF16), P.buf(f"w{tag}{i}"), f"w{i}") for i in range(n)])

    def load_xT(C, src, T, col0=0):
        xT = C.get([128, KC, T], BF16)
        b_x = P.buf("xT")
        v = src.ap().rearrange("(k p) t -> p k t", p=128)
        for q4 in range(4):
            ld('sp', xT[:, q4 * 8:(q4 + 1) * 8, :], v[:, q4 * 8:(q4 + 1) * 8, col0:col0 + T], [b_x], f'x{q4}',
               reads=[b_d[src.name]] if src.name in b_d else [])
        return xT, b_x

    rot8 = Rot(range(8))

    def phase_A1():
        C = carver()
        wdkv = C.get([128, KC, 640], BF16)
        b_wdkv = P.buf("wdkv")
        P.op('pool', lambda e: e.dma_start(out=wdkv.rearrange("p k m -> p (k m)"), in_=w_dkv.ap()),
             writes=[b_wdkv], dma='w0')
        posi = C.get([64, NK], I32)
        CC = C.get([64, NK], F32)
        SS = C.get([64, NK], F32)
        t_rnd = C.get([64, NK], F32)
        b_tab = P.buf("tab")
        ld('sp', posi, pos.ap().partition_broadcast(64), [b_tab], 'x0')
        P.op('dve', lambda e: e.tensor_copy(out=CC, in_=posi), reads=[b_tab], writes=[b_tab])
        P.op('dve', lambda e: e.tensor_scalar(out=SS, in0=CC, scalar1=ropec_sb[:, 0:1], scalar2=ropec_sb[:, 2:3],
                                              op0=ALU.mult, op1=ALU.add), reads=[b_tab, b_const], writes=[b_tab])
        P.op('dve', lambda e: e.tensor_scalar(out=CC, in0=CC, scalar1=ropec_sb[:, 0:1], scalar2=ropec_sb[:, 1:2],
                                              op0=ALU.mult, op1=ALU.add), reads=[b_tab, b_const], writes=[b_tab])
        posf = posi.bitcast(F32)
        for tb_ in (CC, SS):
            P.op('dve', lambda e, tb_=tb_: e.tensor_copy(out=posi, in_=tb_), reads=[b_tab], writes=[b_tab])
            P.op('dve', lambda e, tb_=tb_: e.tensor_copy(out=t_rnd, in_=posi), reads=[b_tab], writes=[b_tab])
            P.op('dve', lambda e, tb_=tb_: e.tensor_tensor(out=tb_, in0=tb_, in1=t_rnd, op=ALU.subtract), reads=[b_tab], writes=[b_tab])
            P.op('act', lambda e, tb_=tb_: e.activation(out=tb_, in_=tb_, func=AF.Sin, scale=2.0 * math.pi * 0.999999),
                 reads=[b_tab], writes=[b_tab])
        ld('sp', CSQ[0, :, :], CC[:, EXT0:EXT0 + T0], [b_d["CSQ"]], 'st0', reads=[b_tab])
        ld('sp', CSQ[1, :, :], SS[:, EXT0:EXT0 + T0], [b_d["CSQ"]], 'st1', reads=[b_tab])
        W = 128
        xb = [C.get([128, KC, W], F32) for _ in range(2)]
        b_xb = [P.buf() for _ in range(2)]
        sqhn = [C.get([128, KC, W], BF16) for _ in range(2)]
        b_sqhn = [P.buf() for _ in range(2)]
        rstd_ = [C.get([128, W], F32) for _ in range(2)]
        b_rstd_ = [P.buf() for _ in range(2)]
        ckv_ = [C.get([128, 4, W], F32) for _ in range(2)]
        b_ckv_ = [P.buf() for _ in range(2)]
        sq2_ = [C.get([128, 4, W], BF16) for _ in range(2)]
        b_sq2_ = [P.buf() for _ in range(2)]
        rstd2_ = [C.get([128, W], F32) for _ in range(2)]
        b_rstd2_ = [P.buf() for _ in range(2)]
        cst = [C.get([128, 4, W], BF16) for _ in range(2)]
        b_cst = [P.buf() for _ in range(2)]
        krs = [C.get([64, W], BF16) for _ in range(2)]
        b_krs = [P.buf() for _ in range(2)]
        t1_ = [C.get([64, W], F32) for _ in range(2)]
        t2_ = [C.get([64, W], F32) for _ in range(2)]
        b_t_ = [P.buf() for _ in range(2)]
        NB = NK // W

        def stage1(ib):
            s0 = ib * W
            i2 = ib % 2
            X, bX, H_, bH = xb[i2], b_xb[i2], sqhn[i2], b_sqhn[i2]
            rstd, b_rstd = rstd_[i2], b_rstd_[i2]
            ld('sp', X[:, 0:16, :], xk[ib, :, 0:2048].rearrange("p (k w) -> p k w", k=16), [bX], f'x{1 + 2 * i2}')
            ld('sp', X[:, 16:32, :], xk[ib, :, 2048:4096].rearrange("p (k w) -> p k w", k=16), [bX], f'x{2 + 2 * i2}')
            bk = rot8.next()
            sumsq(X, bX, H_, bH, KC, W, bk)
            rstd_from_ps(bk, 128, W, rstd, b_rstd, 1.0 / D)
            P.op('dve', lambda e: e.tensor_tensor(out=X, in0=X, in1=g_sb[:, 0, :].unsqueeze(2).broadcast_to([128, KC, W]),
                                                   op=ALU.mult), reads=[bX, b_const], writes=[bX])
            P.op('dve', lambda e: e.tensor_tensor(out=H_, in0=X, in1=rstd.unsqueeze(1).broadcast_to([128, KC, W]),
                                                  op=ALU.mult), reads=[bX, b_rstd], writes=[bH])
            a, b = max(s0, EXT0), min(s0 + W, EXT0 + T0)
            if a < b:
                ld('sp', HN0.ap().rearrange("(k p) t -> p k t", p=128)[:, :, a - EXT0:b - EXT0], H_[:, :, a - s0:b - s0],
                   [b_d["HN0"]], f'st{i2}', reads=[bH])

        def stage2a(ib):
            s0 = ib * W
            i2 = ib % 2
            H_, bH = sqhn[i2], b_sqhn[i2]
            ckv, b_ckv = ckv_[i2], b_ckv_[i2]
            t1, t2, b_t = t1_[i2], t2_[i2], b_t_[i2]
            for m in range(4):
                bk = rot8.next()
                for kc in range(KC):
                    P.op('pe', lambda e, bk=bk, m=m, kc=kc: e.matmul(
                        ps[bk][:, 0:W], lhsT=wdkv[:, kc, m * 128:(m + 1) * 128], rhs=H_[:, kc, :],
                        start=(kc == 0), stop=(kc == KC - 1)), reads=[b_wdkv, bH], writes=[b_ps[bk]])
                P.op('act', lambda e, bk=bk, m=m: e.copy(out=ckv[:, m, :], in_=ps[bk][:, 0:W]),
                     reads=[b_ps[bk]], writes=[b_ckv])
            bkA, bkB = rot8.next(), rot8.next()
            for (bk, c0) in ((bkA, 512), (bkB, 576)):
                for kc in range(KC):
                    P.op('pe', lambda e, bk=bk, c0=c0, kc=kc: e.matmul(
                        ps[bk][0:64, 0:W], lhsT=wdkv[:, kc, c0:c0 + 64], rhs=H_[:, kc, :],
                        start=(kc == 0), stop=(kc == KC - 1)), reads=[b_wdkv, bH], writes=[b_ps[bk]])
            P.op('dve', lambda e: e.tensor_tensor(out=t1, in0=ps[bkA][0:64, 0:W], in1=CC[:, s0:s0 + W],
                                                  op=ALU.mult), reads=[b_ps[bkA], b_tab], writes=[b_t])
            P.op('dve', lambda e: e.tensor_tensor(out=t2, in0=ps[bkB][0:64, 0:W], in1=SS[:, s0:s0 + W],
                                                  op=ALU.mult), reads=[b_ps[bkB], b_tab, b_t], writes=[b_t])
            P.op('dve', lambda e: e.tensor_tensor(out=krs[i2], in0=t1, in1=t2, op=ALU.add),
                 reads=[b_t], writes=[b_krs[i2]])
            ld('sp', KR[:, s0:s0 + W], krs[i2], [b_d["KR"]], f'st{2 + i2}', reads=[b_krs[i2]])

        def stage2b(ib):
            s0 = ib * W
            i2 = ib % 2
            ckv, b_ckv = ckv_[i2], b_ckv_[i2]
            sq2, b_sq2, rstd2, b_rstd2 = sq2_[i2], b_sq2_[i2], rstd2_[i2], b_rstd2_[i2]
            bk = rot8.next()
            sumsq(ckv, b_ckv, sq2, b_sq2, 4, W, bk)
            rstd_from_ps(bk, 128, W, rstd2, b_rstd2, 1.0 / 512)
            P.op('dve', lambda e: e.tensor_tensor(out=ckv, in0=ckv, in1=kvn_sb.unsqueeze(2).broadcast_to([128, 4, W]),
                                                  op=ALU.mult), reads=[b_ckv, b_const], writes=[b_ckv])
            P.op('dve', lambda e: e.tensor_tensor(out=cst[i2], in0=ckv, in1=rstd2.unsqueeze(1).broadcast_to([128, 4, W]),
                                                  op=ALU.mult), reads=[b_ckv, b_rstd2], writes=[b_cst[i2]])
            ld('sp', CT.ap().rearrange("(k p) t -> p k t", p=128)[:, :, s0:s0 + W], cst[i2], [b_d["CT"]], f'st{4 + i2}',
               reads=[b_cst[i2]])

        stage1(0)
        for ib in range(NB):
            if ib + 1 < NB:
                stage1(ib + 1)
            stage2a(ib)
            if ib >= 1:
                stage2b(ib - 1)
        stage2b(NB - 1)
        P.barrier()

    def phase_A2():
        C = carver()
        slots = mk_slots(C, 3, 1, KC, 128, "a2")
        xT, b_x = load_xT(C, HN0, T0)
        cqf = C.get([128, 8, T0], F32)
        b_cqf = P.buf()

        def epi(j, g, t0, w, bank):
            P.op('act', lambda e: e.copy(out=cqf[:, j, t0:t0 + w], in_=ps[bank][:, 0:w]), reads=[b_ps[bank]], writes=[b_cqf])
        stream_gemm(w_dq, 8, 1, KC, 128, lambda kc, t0, w: xT[:, kc, t0:t0 + w], b_x, TB0, epi, None, rot8, slots, "a2")
        sq = C.get([128, 8, 289], BF16)
        b_sq = P.buf()
        rstd = C.get([128, 289], F32)
        b_rstd = P.buf()
        cqb = C.get([128, 8, T0], BF16)
        b_cqb = P.buf()
        for (t0, w) in TB0:
            bk = rot8.next()
            sumsq(cqf[:, :, t0:t0 + w], b_cqf, sq, b_sq, 8, w, bk)
            rstd_from_ps(bk, 128, w, rstd, b_rstd, 1.0 / 1024)
            P.op('dve', lambda e, t0=t0, w=w: e.tensor_tensor(
                out=cqf[:, :, t0:t0 + w], in0=cqf[:, :, t0:t0 + w], in1=qn_sb.unsqueeze(2).broadcast_to([128, 8, w]),
                op=ALU.mult), reads=[b_cqf, b_const], writes=[b_cqf])
            P.op('dve', lambda e, t0=t0, w=w: e.tensor_tensor(
                out=cqb[:, :, t0:t0 + w], in0=cqf[:, :, t0:t0 + w], in1=rstd.unsqueeze(1).broadcast_to([128, 8, w]),
                op=ALU.mult), reads=[b_cqf, b_rstd], writes=[b_cqb])
        ld('sp', CQ.ap().rearrange("(k p) t -> p k t", p=128), cqb, [b_d["CQ"]], 'st0', reads=[b_cqb])
        P.barrier()

    def phase_A3(nheads=32):
        C = carver()
        cT = C.get([128, 4, NK], BF16)
        krT = C.get([64, NK], BF16)
        cqT = C.get([128, 8, T0], BF16)
        CCq = C.get([64, T0], F32)
        SSq = C.get([64, T0], F32)
        b_res = P.buf("a3res")
        ld('sp', cT, CT.ap().rearrange("(k p) t -> p k t", p=128), [b_res], 'x0', reads=[b_d["CT"]])
        ld('sp', krT, KR.ap(), [b_res], 'x1', reads=[b_d["KR"]])
        ld('sp', cqT, CQ.ap().rearrange("(k p) t -> p k t", p=128), [b_res], 'x2', reads=[b_d["CQ"]])
        ld('sp', CCq, CSQ[0, :, :], [b_res], 'x3', reads=[b_d["CSQ"]])
        ld('sp', SSq, CSQ[1, :, :], [b_res], 'x4', reads=[b_d["CSQ"]])
        wq = [C.get([128, 8, 256], BF16) for _ in range(2)]
        b_wq = [P.buf() for _ in range(2)]
        wkv = [C.get([128, 4, 256], BF16) for _ in range(2)]
        b_wkv = [P.buf() for _ in range(2)]
        qn = [C.get([128, T0], BF16) for _ in range(2)]
        b_qn = [P.buf() for _ in range(2)]
        qr = [C.get([64, T0], BF16) for _ in range(2)]
        b_qr = [P.buf() for _ in range(2)]
        knT = [C.get([128, NK], BF16) for _ in range(2)]
        b_kn = [P.buf() for _ in range(2)]
        vsb = [C.get([128, 32, 128], BF16) for _ in range(2)]
        b_v = [P.buf() for _ in range(2)]
        pT = [C.get([128, 512], BF16) for _ in range(4)]
        b_pT = [P.buf() for _ in range(4)]
        aoh = [C.get([128, T0], BF16) for _ in range(2)]
        b_aoh = [P.buf() for _ in range(2)]
        rl = [C.get([128, 512], F32) for _ in range(2)]
        b_rl = [P.buf() for _ in range(2)]
        lacc = [C.get([128, 512], F32) for _ in range(2)]
        b_lacc = [P.buf() for _ in range(2)]
        t1 = C.get([64, T0], F32)
        t2 = C.get([64, T0], F32)
        b_t = P.buf()
        rot4 = Rot(range(4))
        prot = Rot(range(4))
        SCALE = 192 ** -0.5
        QBS = [(0, HALO), (HALO, 512), (HALO + 512, 512)]
        TRI = mask_sb[:, 0:128]
        M6 = mask_sb[:, 128:128 + HALO]
        M7 = mask_sb[:, 128 + HALO:128 + 2 * HALO]
        qbi = 0
        def load_head_w(h):
            hb = h % 2
            P.op('pool', lambda e: e.dma_start(out=wq[hb].rearrange("p k m -> p (k m)"), in_=w_uq[h, :, :]),
                 writes=[b_wq[hb]], dma=f'w{hb}')
            P.op('pool', lambda e: e.dma_start(out=wkv[hb].rearrange("p k m -> p (k m)"), in_=w_ukv[h, :, :]),
                 writes=[b_wkv[hb]], dma=f'w{2 + hb}')
        load_head_w(0)
        for h in range(nheads):
            hb = h % 2
            for (t0, w) in TB0:
                bk = rot4.next()
                for kc in range(8):
                    P.op('pe', lambda e, bk=bk, kc=kc, t0=t0, w=w, hb=hb: e.matmul(
                        ps[bk][:, 0:w], lhsT=wq[hb][:, kc, 0:128], rhs=cqT[:, kc, t0:t0 + w],
                        start=(kc == 0), stop=(kc == 7)), reads=[b_wq[hb], b_res], writes=[b_ps[bk]])
                P.op('act', lambda e, bk=bk, t0=t0, w=w, hb=hb: e.copy(out=qn[hb][:, t0:t0 + w], in_=ps[bk][:, 0:w]),
                     reads=[b_ps[bk]], writes=[b_qn[hb]])
            for (t0, w) in TB0:
                bkA, bkB = rot4.next(), rot4.next()
                for (bk, c0) in ((bkA, 128), (bkB, 192)):
                    for kc in range(8):
                        P.op('pe', lambda e, bk=bk, c0=c0, kc=kc, t0=t0, w=w, hb=hb: e.matmul(
                            ps[bk][0:64, 0:w], lhsT=wq[hb][:, kc, c0:c0 + 64], rhs=cqT[:, kc, t0:t0 + w],
                            start=(kc == 0), stop=(kc == 7)), reads=[b_wq[hb], b_res], writes=[b_ps[bk]])
                P.op('dve', lambda e, bkA=bkA, t0=t0, w=w: e.tensor_tensor(
                    out=t1[:, t0:t0 + w], in0=ps[bkA][0:64, 0:w], in1=CCq[:, t0:t0 + w], op=ALU.mult),
                    reads=[b_ps[bkA], b_res], writes=[b_t])
                P.op('dve', lambda e, bkB=bkB, t0=t0, w=w: e.tensor_tensor(
                    out=t2[:, t0:t0 + w], in0=ps[bkB][0:64, 0:w], in1=SSq[:, t0:t0 + w], op=ALU.mult),
                    reads=[b_ps[bkB], b_res, b_t], writes=[b_t])
            P.op('dve', lambda e, hb=hb: e.tensor_tensor(out=qr[hb], in0=t1, in1=t2, op=ALU.add),
                 reads=[b_t], writes=[b_qr[hb]])
            for kb in range(8):
                bk = rot4.next()
                for kc in range(4):
                    P.op('pe', lambda e, bk=bk, kc=kc, kb=kb, hb=hb: e.matmul(
                        ps[bk][:, 0:512], lhsT=wkv[hb][:, kc, 0:128], rhs=cT[:, kc, kb * 512:(kb + 1) * 512],
                        start=(kc == 0), stop=(kc == 3)), reads=[b_wkv[hb], b_res], writes=[b_ps[bk]])
                eng = 'act' if kb % 2 == 0 else 'dve'
                if eng == 'act':
                    P.op('act', lambda e, bk=bk, kb=kb, hb=hb: e.copy(out=knT[hb][:, kb * 512:(kb + 1) * 512], in_=ps[bk][:, 0:512]),
                         reads=[b_ps[bk]], writes=[b_kn[hb]])
                else:
                    P.op('dve', lambda e, bk=bk, kb=kb, hb=hb: e.tensor_copy(out=knT[hb][:, kb * 512:(kb + 1) * 512], in_=ps[bk][:, 0:512]),
                         reads=[b_ps[bk]], writes=[b_kn[hb]])
            for k4 in range(8):
                bk = rot4.next()
                for i4 in range(4):
                    kt = k4 * 4 + i4
                    for kc in range(4):
                        P.op('pe', lambda e, bk=bk, kc=kc, kt=kt, i4=i4, hb=hb: e.matmul(
                            ps[bk][:, i4 * 128:(i4 + 1) * 128], lhsT=cT[:, kc, kt * 128:(kt + 1) * 128], rhs=wkv[hb][:, kc, 128:256],
                            start=(kc == 0), stop=(kc == 3)), reads=[b_wkv[hb], b_res], writes=[b_ps[bk]])
                if k4 % 2 == 0:
                    P.op('dve', lambda e, bk=bk, k4=k4, hb=hb: e.tensor_copy(
                        out=vsb[hb][:, k4 * 4:(k4 + 1) * 4, :].rearrange("p a b -> p (a b)"), in_=ps[bk][:, 0:512]),
                        reads=[b_ps[bk]], writes=[b_v[hb]])
                else:
                    P.op('act', lambda e, bk=bk, k4=k4, hb=hb: e.copy(
                        out=vsb[hb][:, k4 * 4:(k4 + 1) * 4, :].rearrange("p a b -> p (a b)"), in_=ps[bk][:, 0:512]),
                        reads=[b_ps[bk]], writes=[b_v[hb]])
            if h + 1 < nheads:
                load_head_w(h + 1)
            for qi, (q0, qw) in enumerate(QBS):
                bo = 4 + (qbi % 2)
                bl = 6 + (qbi % 2)
                ri = qbi % 2
                qbi += 1
                tiles = []
                if qi == 0:
                    tiles += [(kt, 0, None) for kt in range(0, 6)] + [(kt, 0, None) for kt in range(16, 32)]
                    tiles += [(6, 0, ('full', M6)), (7, 0, ('full', M7))]
                elif qi == 1:
                    tiles += [(kt, 0, None) for kt in range(0, 8)] + [(kt, 0, None) for kt in range(16, 32)]
                    tiles += [(8 + j, 128 * j, ('tri', TRI)) for j in range(4)]
                else:
                    tiles += [(kt, 0, None) for kt in range(0, 12)] + [(kt, 0, None) for kt in range(16, 32)]
                    tiles += [(12 + j, 128 * j, ('tri', TRI)) for j in range(4)]
                nt = len(tiles)
                LOOK = 3
                binfo = {}

                def emit_qk(ti):
                    kt, c0, mk = tiles[ti]
                    bk = rot4.next()
                    pi = prot.next()
                    binfo[ti] = (bk, pi)
                    P.op('pe', lambda e, bk=bk, kt=kt, c0=c0, q0=q0, qw=qw, hb=hb: e.matmul(
                        ps[bk][:, c0:qw], lhsT=knT[hb][:, kt * 128:(kt + 1) * 128], rhs=qn[hb][:, q0 + c0:q0 + qw],
                        start=True, stop=False), reads=[b_kn[hb], b_qn[hb]], writes=[b_ps[bk]])
                    P.op('pe', lambda e, bk=bk, kt=kt, c0=c0, q0=q0, qw=qw, hb=hb: e.matmul(
                        ps[bk][:, c0:qw], lhsT=krT[:, kt * 128:(kt + 1) * 128], rhs=qr[hb][:, q0 + c0:q0 + qw],
                        start=False, stop=True), reads=[b_res, b_qr[hb]], writes=[b_ps[bk]])
                for ti in range(min(LOOK, nt)):
                    emit_qk(ti)
                for ti, (kt, c0, mk) in enumerate(tiles):
                    if ti + LOOK < nt:
                        emit_qk(ti + LOOK)
                    bk, pi = binfo[ti]
                    P.op('act', lambda e, bk=bk, pi=pi, kt=kt, c0=c0, qw=qw: e.activation(
                        out=pT[pi][:, c0:qw], in_=ps[bk][:, c0:qw], func=AF.Exp, bias=kb0_sb[:, kt:kt + 1], scale=SCALE),
                        reads=[b_ps[bk], b_const], writes=[b_pT[pi]])
                    if mk is not None:
                        mw = qw if mk[0] == 'full' else 128
                        P.op('dve', lambda e, pi=pi, c0=c0, mw=mw, mk=mk: e.tensor_tensor(
                            out=pT[pi][:, c0:c0 + mw], in0=pT[pi][:, c0:c0 + mw], in1=mk[1][:, 0:mw], op=ALU.mult),
                            reads=[b_pT[pi], b_const], writes=[b_pT[pi]])
                    P.op('pe', lambda e, pi=pi, kt=kt, c0=c0, ti=ti, qw=qw, hb=hb, bo=bo, nt=nt: e.matmul(
                        ps[bo][:, c0:qw], lhsT=vsb[hb][:, kt, :], rhs=pT[pi][:, c0:qw],
                        start=(ti == 0), stop=(ti == nt - 1)), reads=[b_v[hb], b_pT[pi]], writes=[b_ps[bo]])
                    P.op('pe', lambda e, pi=pi, c0=c0, ti=ti, qw=qw, bl=bl, nt=nt: e.matmul(
                        ps[bl][:, c0:qw], lhsT=ones_bf, rhs=pT[pi][:, c0:qw],
                        start=(ti == 0), stop=(ti == nt - 1)), reads=[b_const, b_pT[pi]], writes=[b_ps[bl]])
                P.op('dve', lambda e, bl=bl, ri=ri, qw=qw: e.tensor_scalar(
                    out=rl[ri][:, 0:qw], in0=ps[bl][:, 0:qw], scalar1=1e-30, scalar2=None, op0=ALU.add),
                    reads=[b_ps[bl]], writes=[b_rl[ri]])
                P.op('dve', lambda e, ri=ri, qw=qw: e.reciprocal(out=rl[ri][:, 0:qw], in_=rl[ri][:, 0:qw]),
                     reads=[b_rl[ri]], writes=[b_rl[ri]])
                P.op('dve', lambda e, bo=bo, ri=ri, q0=q0, qw=qw, hb=hb: e.tensor_tensor(
                    out=aoh[hb][:, q0:q0 + qw], in0=ps[bo][:, 0:qw], in1=rl[ri][:, 0:qw], op=ALU.mult),
                    reads=[b_ps[bo], b_rl[ri]], writes=[b_aoh[hb]])
            ld('sp', AO[h * 128:(h + 1) * 128, :], aoh[hb], [b_d["AO"]], f'st{hb}', reads=[b_aoh[hb]])
        P.barrier()

    def phase_proj(wd, src, T, tbs, col0=0):
        C = carver()
        slots = mk_slots(C, 3, 1, KC, 128, "pj")
        xT, b_x = load_xT(C, src, T, col0)
        yst = [C.get([128, T], F32) for _ in range(2)]
        b_yst = [P.buf() for _ in range(2)]

        def epi(j, g, t0, w, bank):
            P.op('act', lambda e: e.copy(out=yst[j % 2][:, t0:t0 + w], in_=ps[bank][:, 0:w]),
                 reads=[b_ps[bank]], writes=[b_yst[j % 2]])

        def post(j):
            ld('sp', Y[j * 128:(j + 1) * 128, 0:T], yst[j % 2], [b_d["Y"]], f'st{j % 2}', reads=[b_yst[j % 2]])
        stream_gemm(wd, 32, 1, KC, 128, lambda kc, t0, w: xT[:, kc, t0:t0 + w], b_x, tbs, epi, post, rot8, slots, "pj")
        P.barrier()

    def phase_norm(T, tbs, hsrc, hcol0, g_post, hdst, outs, flagcols=0, final=False):
        C = carver()
        W = tbs[0][1]
        yb = C.get([128, KC, W], F32)
        b_yb = P.buf()
        hb = C.get([128, KC, W], F32)
        b_hb = P.buf()
        sq = C.get([128, KC, W], BF16)
        b_sq = P.buf()
        hnb = [C.get([128, KC, W], BF16) for _ in range(2)]
        b_hnb = [P.buf() for _ in range(2)]
        rstd = C.get([128, W], F32)
        b_rstd = P.buf()
        rstd2 = C.get([128, W], F32)
        b_rstd2 = P.buf()
        yv = Y.ap().rearrange("(k p) t -> p k t", p=128)
        is_x = (hsrc is None)
        hv = None if is_x else hsrc.ap().rearrange("(k p) t -> p k t", p=128)
        hsname = None if is_x else (hsrc.name if hsrc.name in b_d else None)
        oi = 0
        for (t0, w) in tbs:
            for q2 in range(2):
                ld('sp', yb[:, q2 * 16:(q2 + 1) * 16, 0:w], yv[:, q2 * 16:(q2 + 1) * 16, t0:t0 + w], [b_yb], f'x{q2}', reads=[b_d["Y"]])
                if not is_x:
                    ld('sp', hb[:, q2 * 16:(q2 + 1) * 16, 0:w], hv[:, q2 * 16:(q2 + 1) * 16, hcol0 + t0:hcol0 + t0 + w], [b_hb], f'x{2 + q2}',
                       reads=[b_d[hsname]] if hsname else [])
            if is_x:
                A_, B_ = hcol0 + t0, hcol0 + t0 + w
                for bi_, blk in enumerate(range(A_ // 128, (B_ - 1) // 128 + 1)):
                    lo, hi = max(A_, blk * 128), min(B_, (blk + 1) * 128)
                    ld('sp', hb[:, :, lo - A_:hi - A_],
                       xk[blk, :, :].rearrange("p (k w) -> p k w", k=KC)[:, :, lo - blk * 128:hi - blk * 128], [b_hb], f'x{2 + bi_ % 2}')
            bk = rot8.next()
            sumsq(yb[:, :, 0:w], b_yb, sq[:, :, 0:w], b_sq, KC, w, bk)
            rstd_from_ps(bk, 128, w, rstd[:, 0:w], b_rstd, 1.0 / D)
            P.op('dve', lambda e, w=w: e.tensor_tensor(out=yb[:, :, 0:w], in0=yb[:, :, 0:w],
                                                       in1=g_sb[:, g_post, :].unsqueeze(2).broadcast_to([128, KC, w]), op=ALU.mult),
                 reads=[b_yb, b_const], writes=[b_yb])
            P.op('dve', lambda e, w=w: e.tensor_tensor(out=yb[:, :, 0:w], in0=yb[:, :, 0:w],
                                                       in1=rstd[:, 0:w].unsqueeze(1).broadcast_to([128, KC, w]), op=ALU.mult),
                 reads=[b_yb, b_rstd], writes=[b_yb])
            P.op('dve', lambda e, w=w: e.tensor_tensor(out=hb[:, :, 0:w], in0=hb[:, :, 0:w], in1=yb[:, :, 0:w], op=ALU.add),
                 reads=[b_yb, b_hb], writes=[b_hb])
            if flagcols > t0:
                fw_ = min(flagcols, t0 + w) - t0
                P.op('dve', lambda e, fw_=fw_: e.tensor_scalar(out=hb[:, :, 0:fw_], in0=hb[:, :, 0:fw_], scalar1=flag_sb[:, 0:1],
                                                               scalar2=None, op0=ALU.mult), reads=[b_hb, b_const], writes=[b_hb])
            if final:
                a, b = max(t0, OWN1), t0 + w
                ld('sp', out_d.ap().rearrange("(k p) t -> p k t", p=128)[:, :, a - OWN1:b - OWN1], hb[:, :, a - t0:b - t0],
                   [b_d["out"]], 'st0', reads=[b_hb])
                continue
            ld('sp', hdst.ap().rearrange("(k p) t -> p k t", p=128)[:, :, t0:t0 + w], hb[:, :, 0:w], [b_d[hdst.name]], 'st0', reads=[b_hb])
            bk = rot8.next()
            sumsq(hb[:, :, 0:w], b_hb, sq[:, :, 0:w], b_sq, KC, w, bk)
            rstd_from_ps(bk, 128, w, rstd2[:, 0:w], b_rstd2, 1.0 / D)
            for (gi, dst) in outs:
                o2 = oi % 2
                oi += 1
                P.op('dve', lambda e, w=w, gi=gi: e.tensor_tensor(out=yb[:, :, 0:w], in0=hb[:, :, 0:w],
                                                               in1=g_sb[:, gi, :].unsqueeze(2).broadcast_to([128, KC, w]), op=ALU.mult),
                     reads=[b_hb, b_const], writes=[b_yb])
                P.op('dve', lambda e, w=w, o2=o2: e.tensor_tensor(out=hnb[o2][:, :, 0:w], in0=yb[:, :, 0:w],
                                                               in1=rstd2[:, 0:w].unsqueeze(1).broadcast_to([128, KC, w]), op=ALU.mult),
                     reads=[b_yb, b_rstd2], writes=[b_hnb[o2]])
                ld('sp', dst.ap().rearrange("(k p) t -> p k t", p=128)[:, :, t0:t0 + w], hnb[o2][:, :, 0:w], [b_d[dst.name]],
                   f'st{1 + o2}', reads=[b_hnb[o2]])
        P.barrier()

    def phase_F1(l, src, T, tbs, col0=0):
        C = carver()
        slots = mk_slots(C, 3, 2, KC, 128, "f1")
        xT, b_x = load_xT(C, src, T, col0)
        cw_sb = C.get([128, 3, 2 * NJ], F32)
        cb_sb = C.get([128, 2 * NJ], F32)
        b_c = P.buf()
        ld('sp', cw_sb, cw[l].ap(), [b_c], 'x4')
        ld('sp', cb_sb, cb[l].ap(), [b_c], 'x5')
        u = [[C.get([128, T], F32) for _ in range(2)] for _ in range(2)]
        b_u = [[P.buf() for _ in range(2)] for _ in range(2)]
        cv = [[C.get([128, T], F32) for _ in range(2)] for _ in range(2)]
        b_cv = [[P.buf() for _ in range(2)] for _ in range(2)]
        ao = [C.get([128, T], BF16) for _ in range(2)]
        b_ao = [P.buf() for _ in range(2)]
        for i in range(2):
            P.op('dve', lambda e, i=i: e.memset(ao[i][:, 0:2], 0.0), writes=[b_ao[i]])

        def epi(j, g, t0, w, bank):
            ub = j % 2
            P.op('act', lambda e: e.copy(out=u[g][ub][:, t0:t0 + w], in_=ps[bank][:, 0:w]), reads=[b_ps[bank]], writes=[b_u[g][ub]])
            if t0 + w == T:
                blk = g * NJ + j
                U, Cv = u[g][ub], cv[g][ub]
                eng = 'dve'
                P.op(eng, lambda e: e.tensor_scalar(out=Cv[:, 2:T], in0=U[:, 2:T], scalar1=cw_sb[:, 2, blk:blk + 1],
                                                    scalar2=cb_sb[:, blk:blk + 1], op0=ALU.mult, op1=ALU.add),
                     reads=[b_u[g][ub], b_c], writes=[b_cv[g][ub]])
                P.op(eng, lambda e: e.scalar_tensor_tensor(out=Cv[:, 2:T], in0=U[:, 1:T - 1], scalar=cw_sb[:, 1, blk:blk + 1],
                                                           in1=Cv[:, 2:T], op0=ALU.mult, op1=ALU.add),
                     reads=[b_u[g][ub], b_c, b_cv[g][ub]], writes=[b_cv[g][ub]])
                P.op(eng, lambda e: e.scalar_tensor_tensor(out=Cv[:, 2:T], in0=U[:, 0:T - 2], scalar=cw_sb[:, 0, blk:blk + 1],
                                                           in1=Cv[:, 2:T], op0=ALU.mult, op1=ALU.add),
                     reads=[b_u[g][ub], b_c, b_cv[g][ub]], writes=[b_cv[g][ub]])

        def post(j):
            ub = j % 2
            P.op('act', lambda e: e.activation(out=cv[0][ub][:, 2:T], in_=cv[0][ub][:, 2:T], func=AF.Silu),
                 reads=[b_cv[0][ub]], writes=[b_cv[0][ub]])
            P.op('dve', lambda e: e.tensor_tensor(out=ao[ub][:, 2:T], in0=cv[0][ub][:, 2:T], in1=cv[1][ub][:, 2:T], op=ALU.mult),
                 reads=[b_cv[0][ub], b_cv[1][ub]], writes=[b_ao[ub]])
            ld('sp', ACT[j, :, 0:T], ao[ub], [b_d["ACT"]], f'st{ub}', reads=[b_ao[ub]])
        stream_gemm(w_in[l], NJ, 2, KC, 128, lambda kc, t0, w: xT[:, kc, t0:t0 + w], b_x, tbs, epi, post, rot8, slots, "f1")
        P.barrier()

    def phase_F2(l, T, tbs):
        W = tbs[0][1]
        for half in range(2):
            C = carver()
            slots = mk_slots(C, 3, 1, NJ, 128, "f2")
            hT = 2 * W
            h0 = half * hT
            aT = C.get([128, NJ, hT], BF16)
            b_a = P.buf()
            av = ACT.ap().rearrange("j p t -> p j t")
            nq = 6
            for q in range(nq):
                j0, j1 = q * NJ // nq, (q + 1) * NJ // nq
                ld('sp', aT[:, j0:j1, :], av[:, j0:j1, h0:h0 + hT], [b_a], f'x{q}', reads=[b_d["ACT"]])
            yst = [C.get([128, hT], F32) for _ in range(2)]
            b_yst = [P.buf() for _ in range(2)]

            def epi(j, g, t0, w, bank, yst=yst, b_yst=b_yst):
                P.op('act', lambda e: e.copy(out=yst[j % 2][:, t0:t0 + w], in_=ps[bank][:, 0:w]),
                     reads=[b_ps[bank]], writes=[b_yst[j % 2]])

            def post(j, yst=yst, b_yst=b_yst, h0=h0, hT=hT):
                ld('sp', Y[j * 128:(j + 1) * 128, h0:h0 + hT], yst[j % 2], [b_d["Y"]], f'st{j % 2}', reads=[b_yst[j % 2]])
            stream_gemm(w_out[l], 32, 1, NJ, 128, lambda kc, t0, w, aT=aT: aT[:, kc, t0:t0 + w], b_a,
                        [(0, W), (W, W)], epi, post, rot8, slots, "f2")
            P.barrier()

    def phase_B1():
        C = carver()
        slots = mk_slots(C, 3, 1, KC, 128, "b1")
        xT, b_x = load_xT(C, HKV, T0)
        wv = C.get([128, KC, 512], BF16)
        b_wv = P.buf()
        for q4 in range(4):
            if variant in (3, 4):
                continue
            P.op('pool', lambda e, q4=q4: e.dma_start(out=wv[:, q4 * 8:(q4 + 1) * 8, :].rearrange("p k m -> p (k m)"),
                                                      in_=w_v.ap()[:, q4 * 4096:(q4 + 1) * 4096]), writes=[b_wv], dma='w3')
        kst = [C.get([128, T0], BF16) for _ in range(2)]
        b_kst = [P.buf() for _ in range(2)]

        def epi(j, g, t0, w, bank):
            P.op('act', lambda e: e.copy(out=kst[j % 2][:, t0:t0 + w], in_=ps[bank][:, 0:w]), reads=[b_ps[bank]], writes=[b_kst[j % 2]])

        def post(j):
            ld('sp', KD[j, :, :], kst[j % 2], [b_d["KD"]], f'st{j % 2}', reads=[b_kst[j % 2]])
        stream_gemm(w_k, 8, 1, KC, 128, lambda kc, t0, w: xT[:, kc, t0:t0 + w], b_x, TB0, epi, post, rot8, slots, "b1")
        vl = [C.get([128, 8, 128], BF16) for _ in range(2)]
        vr = [C.get([128, 8, 128], BF16) for _ in range(2)]
        b_vl = [P.buf() for _ in range(2)]
        b_vr = [P.buf() for _ in range(2)]
        for i in range(2):
            if variant in (3, 5):
                continue
            P.op('dve', lambda e, i=i: e.memset(vl[i], 0.0), writes=[b_vl[i]])
            P.op('dve', lambda e, i=i: e.memset(vr[i], 0.0), writes=[b_vr[i]])
        for tt in range(10):
            if variant in (1, 3, 4, 5) or (variant == 2 and tt == 9):
                continue
            k0 = tt * 128
            nk = min(128, T0 - k0)
            bk = rot8.next()
            i2 = tt % 2
            for kc in range(KC):
                P.op('pe', lambda e, bk=bk, kc=kc, k0=k0, nk=nk: e.matmul(
                    ps[bk][0:nk, 0:512], lhsT=xT[:, kc, k0:k0 + nk], rhs=wv[:, kc, :],
                    start=(kc == 0), stop=(kc == KC - 1)), reads=[b_x, b_wv], writes=[b_ps[bk]])
            P.op('act', lambda e, bk=bk, nk=nk, i2=i2: e.copy(
                out=vl[i2][0:nk, :, 0:64], in_=ps[bk][0:nk, 0:512].rearrange("p (g d) -> p g d", g=8)),
                reads=[b_ps[bk]], writes=[b_vl[i2]])
            P.op('dve', lambda e, nk=nk, i2=i2: e.tensor_copy(out=vr[i2][0:nk, :, 64:128], in_=vl[i2][0:nk, :, 0:64]),
                 reads=[b_vl[i2]], writes=[b_vr[i2]])
            if variant == 6:
                continue
            ld('sp', VL[tt, :, :], vl[i2].rearrange("p g d -> p (g d)"), [b_d["VL"]], f'st{2 + i2}', reads=[b_vl[i2]])
            ld('sp', VR[tt, :, :], vr[i2].rearrange("p g d -> p (g d)"), [b_d["VR"]], f'st{4 + i2}', reads=[b_vr[i2]])
        P.barrier()

    def phase_B2(npairs=32):
        C = carver()
        slots = mk_slots(C, 3, 1, KC, 128, "b2")
        xT, b_x = load_xT(C, HNQ, T1, 128)
        kT = C.get([128, 8, T0], BF16)
        vL = C.get([128, 10, 8, 128], BF16)
        vR = C.get([128, 10, 8, 128], BF16)
        b_kv = P.buf()
        ld('sp', kT, KD.ap().rearrange("g p t -> p g t"), [b_kv], 'x4', reads=[b_d["KD"]])
        ld('sp', vL.rearrange("p t g d -> p t (g d)"), VL.ap().rearrange("t p f -> p t f"), [b_kv], 'x5', reads=[b_d["VL"]])
        ld('sp', vR.rearrange("p t g d -> p t (g d)"), VR.ap().rearrange("t p f -> p t f"), [b_kv], 'x6', reads=[b_d["VR"]])
        q2 = [C.get([128, T1], BF16) for _ in range(2)]
        b_q2 = [P.buf() for _ in range(2)]
        ehs = [C.get([128, 2, 256], F32) for _ in range(2)]
        b_eh = [P.buf() for _ in range(2)]
        pex = [C.get([128, 256], F32) for _ in range(4)]
        b_pex = [P.buf() for _ in range(4)]
        pT = [C.get([128, 256], BF16) for _ in range(4)]
        b_pT = [P.buf() for _ in range(4)]
        ao2 = [C.get([128, T1], BF16) for _ in range(2)]
        b_ao2 = [P.buf() for _ in range(2)]
        rl = [C.get([128, 128], F32) for _ in range(2)]
        b_rl = [P.buf() for _ in range(2)]
        rot4 = Rot(range(4))
        SC = 64 ** -0.5
        cnt = [0, 0]

        def epi(j, g, t0, w, bank):
            P.op('act', lambda e: e.copy(out=q2[j % 2][:, t0:t0 + w], in_=ps[bank][:, 0:w]), reads=[b_ps[bank]], writes=[b_q2[j % 2]])

        def post(m):
            m2 = m % 2
            ld('sp', ehs[m2], eh[m, :, :, :], [b_eh[m2]], f'x{7 + m2}')
            g = (2 * m) // 8
            units = [(qt, hh) for qt in range(9) for hh in range(2)]
            info = {}
            acc = {}

            def geom(qt):
                qa = 128 + qt * 128
                nq = min(128, T0 - qa)
                return qa, nq, nq

            def stA(u):
                qt, hh = units[u]
                qa, nq, nkc = geom(qt)
                pb = 64 * hh
                bk = rot4.next()
                pi = cnt[1] % 4
                cnt[1] += 1
                info[u] = (bk, pi)
                P.op('pe', lambda e: e.matmul(
                    ps[bk][:, 0:nq], lhsT=kT[pb:pb + 64, g, qa - 128:qa], rhs=q2[m2][pb:pb + 64, qa - 128:qa - 128 + nq],
                    start=True, stop=True), reads=[b_kv, b_q2[m2]], writes=[b_ps[bk]])
                P.op('pe', lambda e: e.matmul(
                    ps[bk][0:nkc, 128:128 + nq], lhsT=kT[pb:pb + 64, g, qa:qa + nkc], rhs=q2[m2][pb:pb + 64, qa - 128:qa - 128 + nq],
                    start=True, stop=True), reads=[b_kv, b_q2[m2]], writes=[b_ps[bk]])

            def stB(u):
                qt, hh = units[u]
                qa, nq, nkc = geom(qt)
                bk, pi = info[u]
                kprev = qt
                P.op('act', lambda e: e.activation(
                    out=pex[pi][:, 0:nq], in_=ps[bk][:, 0:nq], func=AF.Exp, bias=kb1_sb[:, kprev:kprev + 1], scale=SC),
                    reads=[b_ps[bk], b_const], writes=[b_pex[pi]])
                P.op('act', lambda e: e.activation(
                    out=pex[pi][0:nkc, 128:128 + nq], in_=ps[bk][0:nkc, 128:128 + nq], func=AF.Exp,
                    bias=kb1_sb[0:nkc, kprev + 1:kprev + 2], scale=SC),
                    reads=[b_ps[bk], b_const], writes=[b_pex[pi]])
                P.op('dve', lambda e: e.tensor_tensor(
                    out=pT[pi][:, 0:nq], in0=pex[pi][:, 0:nq], in1=ehs[m2][:, hh, 0:nq], op=ALU.mult),
                    reads=[b_pex[pi], b_eh[m2]], writes=[b_pT[pi]])
                P.op('dve', lambda e: e.tensor_tensor(
                    out=pT[pi][0:nkc, 128:128 + nq], in0=pex[pi][0:nkc, 128:128 + nq], in1=ehs[m2][0:nkc, hh, 128:128 + nq], op=ALU.mult),
                    reads=[b_pex[pi], b_eh[m2]], writes=[b_pT[pi]])

            def stC(u):
                qt, hh = units[u]
                qa, nq, nkc = geom(qt)
                bk, pi = info[u]
                kprev = qt
                if hh == 0:
                    acc[qt] = (4 + (cnt[0] % 2), 6 + (cnt[0] % 2), cnt[0] % 2)
                    cnt[0] += 1
                bo, bl, ri = acc[qt]
                vv = vL if hh == 0 else vR
                on = onesL if hh == 0 else onesR
                for ii, (kti, c0, nkk) in enumerate(((kprev, 0, 128), (kprev + 1, 128, nkc))):
                    nmm = 2 * hh + ii
                    P.op('pe', lambda e, kti=kti, c0=c0, nkk=nkk, nmm=nmm: e.matmul(
                        ps[bo][:, 0:nq], lhsT=vv[0:nkk, kti, g, :], rhs=pT[pi][0:nkk, c0:c0 + nq],
                        start=(nmm == 0), stop=(nmm == 3)), reads=[b_kv, b_pT[pi]], writes=[b_ps[bo]])
                    P.op('pe', lambda e, c0=c0, nkk=nkk, nmm=nmm: e.matmul(
                        ps[bl][:, 0:nq], lhsT=on[0:nkk, :], rhs=pT[pi][0:nkk, c0:c0 + nq],
                        start=(nmm == 0), stop=(nmm == 3)), reads=[b_const, b_pT[pi]], writes=[b_ps[bl]])
                if hh == 1:
                    P.op('dve', lambda e: e.tensor_scalar(
                        out=rl[ri][:, 0:nq], in0=ps[bl][:, 0:nq], scalar1=esink[:, m:m + 1], scalar2=None, op0=ALU.add),
                        reads=[b_ps[bl], b_const], writes=[b_rl[ri]])
                    P.op('dve', lambda e: e.reciprocal(out=rl[ri][:, 0:nq], in_=rl[ri][:, 0:nq]),
                         reads=[b_rl[ri]], writes=[b_rl[ri]])
                    P.op('dve', lambda e: e.tensor_tensor(
                        out=ao2[m2][:, qa - 128:qa - 128 + nq], in0=ps[bo][:, 0:nq], in1=rl[ri][:, 0:nq], op=ALU.mult),
                        reads=[b_ps[bo], b_rl[ri]], writes=[b_ao2[m2]])
            LOOK = 2
            nu = len(units)
            for u in range(min(LOOK, nu)):
                stA(u)
            for u in range(nu):
                if u + LOOK < nu:
                    stA(u + LOOK)
                stB(u)
                stC(u)
            ld('sp', AO[m * 128:(m + 1) * 128, 0:T1], ao2[m2], [b_d["AO"]], f'st{m2}', reads=[b_ao2[m2]])
        stream_gemm(w_q, npairs, 1, KC, 128, lambda kc, t0, w: xT[:, kc, t0:t0 + w], b_x, TB1, epi, post, rot4, slots, "b2")
        P.barrier()

    class XE:
        name = "xk"
        @staticmethod
        def ap():
            return xk.ap()
    phases = [
        ("A1", lambda: phase_A1()),
        ("A2", lambda: phase_A2()),
        ("A3", lambda: phase_A3()),
        ("A4", lambda: phase_proj(w_o0, AO, T0, TB0)),
        ("A5", lambda: phase_norm(T0, TB0, None, EXT0, 1, HM0, [(2, HN)])),
        ("F1a", lambda: phase_F1(0, HN, T0, TB0)),
        ("F2a", lambda: phase_F2(0, T0, TB0)),
        ("F3a", lambda: phase_norm(T0, TB0, HM0, 0, 3, H1, [(4, HNQ), (8, HKV)], flagcols=HALO)),
        ("B1", lambda: phase_B1()),
        ("B2", lambda: phase_B2()),
        ("B3", lambda: phase_proj(w_o1, AO, T1, TB1)),
        ("B4", lambda: phase_norm(T1, TB1, H1, 128, 5, HM1, [(6, HN)])),
        ("F1b", lambda: phase_F1(1, HN, T1, TB1)),
        ("F2b", lambda: phase_F2(1, T1, TB1)),
        ("F3b", lambda: phase_norm(T1, TB1, HM1, 0, 7, None, [], final=True)),
    ]
    for name, fn in phases:
        if name in run_list:
            fn()
    P.finish('sp')
    P.emit()
    return nc


def tile_w(W, mcols):
    Kd, N = W.shape
    return np.ascontiguousarray(
        W.reshape(Kd // 128, 128, N // mcols, mcols).transpose(2, 1, 0, 3)).reshape(N // mcols, 128, (Kd // 128) * mcols)


def gt(g):
    return np.ascontiguousarray(g.reshape(-1, 128).T)


def prep_shared(inp):
    import ml_dtypes
    f = np.float32
    sh = {}
    gains = np.stack([gt(inp["norm_mix_pre"][0]), gt(inp["norm_mix_post"][0]), gt(inp["norm_ffn_pre"][0]),
                      gt(inp["norm_ffn_post"][0]), gt(inp["norm_mix_pre"][1]), gt(inp["norm_mix_post"][1]),
                      gt(inp["norm_ffn_pre"][1]), gt(inp["norm_ffn_post"][1]), gt(inp["shared_kv_norm"])], axis=1)
    sh["gains"] = np.ascontiguousarray(gains.astype(f))
    sh["qn_g"] = gt(inp["mla_q_norm"][0]).astype(f)
    sh["kvn_g"] = gt(inp["mla_kv_norm"][0]).astype(f)
    wdkv = inp["mla_w_dkv"][0]
    wdkv_e = np.concatenate([wdkv, wdkv[:, 544:576], wdkv[:, 512:544]], axis=1)
    sh["w_dkv"] = tile_w(wdkv_e, 640)[0]
    sh["w_dq"] = tile_w(inp["mla_w_dq"][0], 128)
    wuq = inp["mla_w_uq"][0].reshape(1024, 32, 192)
    wuq_e = np.concatenate([wuq, wuq[:, :, 160:192], wuq[:, :, 128:160]], axis=2).reshape(1024, 32 * 256)
    sh["w_uq"] = tile_w(wuq_e, 256)
    sh["w_ukv"] = tile_w(inp["mla_w_ukv"][0], 256)
    sh["w_o0"] = tile_w(inp["mla_w_o"][0], 128)
    wk = inp["swa_w_k"].reshape(D, 8, 64)
    sh["w_k"] = tile_w(np.concatenate([wk, wk], axis=2).reshape(D, 1024), 128)
    sh["w_v"] = tile_w(inp["swa_w_v"], 512)[0]
    sh["w_q"] = tile_w(inp["swa_w_q"][0], 128)
    sh["w_o1"] = tile_w(inp["swa_w_o"][0], 128)
    for l in range(2):
        wi = inp["ffn_w_in"][l].reshape(KC, 128, 2, NJ, 128)
        sh[f"w_in{l}"] = np.ascontiguousarray(wi.transpose(3, 1, 2, 0, 4)).reshape(NJ, 128, 2 * KC * 128)
        sh[f"w_out{l}"] = tile_w(inp["ffn_w_out"][l], 128)
        sh[f"cw{l}"] = np.ascontiguousarray(inp["ffn_conv_w"][l].reshape(3, 2 * NJ, 128).transpose(2, 0, 1)).astype(f)
        sh[f"cb{l}"] = np.ascontiguousarray(inp["ffn_conv_b"][l].reshape(2 * NJ, 128).T).astype(f)
    s = inp["swa_sinks"][0]
    sk = np.zeros((128, 32), f)
    sk[0:64, :] = s[0::2][None, :]
    sk[64:128, :] = s[1::2][None, :]
    sh["sinks"] = sk
    p = np.arange(64)
    rc = np.zeros((64, 4), f)
    rc[:, 0] = ((10000.0 ** (-(p % 32).astype(np.float64) / 32)) / (2.0 * math.pi)).astype(f)
    rc[:, 1] = 0.25
    rc[:, 2] = np.where(p < 32, 0.5, 0.0)
    rc[:, 3] = 0.0
    sh["ropec"] = rc
    kk = np.arange(128)[:, None]
    tri = (kk <= np.arange(128)[None, :]).astype(f)
    col = np.arange(HALO)[None, :]
    m6 = ((768 + kk) <= (EXT0 + col)).astype(f)
    m7 = ((896 + kk) <= (EXT0 + col)).astype(f)
    sh["masks"] = np.concatenate([tri, m6, m7], axis=1).astype(ml_dtypes.bfloat16)
    slopes = 2.0 ** (-8.0 * np.arange(1, 65, dtype=np.float64) / 64)
    ehv = np.zeros((32, 128, 2, 256), f)
    cc = np.arange(128)[None, :]
    for h in range(64):
        dprev = 128 + cc - kk
        dcur = cc - kk
        ehv[h // 2, :, h % 2, 0:128] = np.where(dprev <= 127, np.exp(-slopes[h] * dprev), 0.0)
        ehv[h // 2, :, h % 2, 128:256] = np.where(dcur >= 0, np.exp(-slopes[h] * dcur), 0.0)
    sh["eh"] = ehv
    return sh


def prep_core(inp, b, c):
    f = np.float32
    x = inp["x"][b]
    posv = inp["positions"][b]
    xkv = np.zeros((D, NK), f)
    pp = np.zeros((NK,), np.int32)
    kb0 = np.zeros((128, 32), f)
    for grp, ch in enumerate([c - 1, c, c - 2, c - 3]):
        if ch >= 0:
            xkv[:, grp * CH:(grp + 1) * CH] = x[ch * CH:(ch + 1) * CH].T
            pp[grp * CH:(grp + 1) * CH] = posv[ch * CH:(ch + 1) * CH]
        else:
            kb0[:, grp * 8:(grp + 1) * 8] = NEG
    kb1 = np.zeros((128, 10), f)
    if c == 0:
        for tt in range(10):
            colv = tt * 128 + np.arange(128)
            kb1[:, tt] = np.where(colv < HALO, NEG, 0.0)
    fl = np.full((128, 1), 1.0 if c > 0 else 0.0, f)
    xkt = np.ascontiguousarray(xkv.reshape(KC, 128, NK // 128, 128).transpose(2, 1, 0, 3)).reshape(NK // 128, 128, KC * 128)
    return {"xk": xkt, "pos": pp, "kbias0": kb0, "kbias1": kb1, "flag": fl}


_CACHE = {}


def kernel(**inputs):
    inp = {k_: np.asarray(v) for k_, v in inputs.items()}
    sh = prep_shared(inp)
    if "nc" not in _CACHE:
        _CACHE["nc"] = build()
    nc = _CACHE["nc"]
    in_maps = []
    for core in range(8):
        b, c = core // 4, core % 4
        m = dict(sh)
        m.update(prep_core(inp, b, c))
        in_maps.append(m)
    res = run_bass_kernel_spmd(nc, in_maps, core_ids=list(range(8)))
    out = np.zeros((2, SEQ, D), np.float32)
    for core in range(8):
        b, c = core // 4, core % 4
        out[b, c * CH:(c + 1) * CH, :] = res.results[core]["out"].T
    return out
```
